# Optimizing a Trainium2 kernel written in Bass

```python
import jax, jax.numpy as jnp
from jax import lax
import numpy as np

D_MODEL = 4096
BATCH = 2
SEQ = 4096
DEPTH = 2

POOL_WIDTH = D_MODEL // 4
POOL_WINDOWS = (2, 4, 8, 16)
POOL_GROUPS = len(POOL_WINDOWS)
POOL_GROUP_DIM = POOL_WIDTH // POOL_GROUPS

SB_HEAD_DIM = 128
SB_WIDTH = D_MODEL // 2
SB_HEADS = SB_WIDTH // SB_HEAD_DIM
SB_BLOCK = 128

RWKV_HEAD_DIM = 64
RWKV_WIDTH = D_MODEL // 4
RWKV_HEADS = RWKV_WIDTH // RWKV_HEAD_DIM
RWKV_DECAY_RANK = max(32, int(round(RWKV_WIDTH ** 0.5 * 1.8 / 32)) * 32)
RWKV_A_RANK = max(32, int(round(RWKV_WIDTH ** 0.5 * 1.8 / 32)) * 32)
RWKV_GATE_RANK = max(32, int(round(RWKV_WIDTH ** 0.8 * 0.6 / 32)) * 32)
RWKV_SHIFT_WIDTH = 3 * RWKV_WIDTH + RWKV_DECAY_RANK + RWKV_A_RANK + RWKV_GATE_RANK
RWKV_GN_EPS = 64e-5

N_BRANCH = 3
SPLIT_POOL = POOL_WIDTH
SPLIT_Q = SPLIT_POOL + SB_WIDTH
SPLIT_K = SPLIT_Q + SB_WIDTH
SPLIT_V = SPLIT_K + SB_WIDTH
SPLIT_RWKV = SPLIT_V + RWKV_SHIFT_WIDTH
N_IN = SPLIT_RWKV + N_BRANCH * D_MODEL

D_FF = ((8 * D_MODEL // 3 + 255) // 256) * 256
CONV_WIDTH = 3

DEEPNORM_ALPHA = (2 * DEPTH) ** 0.25
DEEPNORM_BETA = (8 * DEPTH) ** -0.25
LN_EPS = 1e-5

kernel_name = "hybrid_pool_stickbreak_rwkv7_deepnorm"


def layer_norm(x, g, b, eps=LN_EPS):
    xf = x.astype(jnp.float32)
    mu = jnp.mean(xf, axis=-1, keepdims=True)
    var = jnp.mean(jnp.square(xf - mu), axis=-1, keepdims=True)
    return ((xf - mu) * lax.rsqrt(var + eps) * g + b).astype(x.dtype)


def pool_mixer(u, pool_w, pool_scale):
    B, S, _ = u.shape
    ug = u.reshape(B, S, POOL_GROUPS, POOL_GROUP_DIM).astype(jnp.float32)
    csum = jnp.pad(jnp.cumsum(ug, axis=1), ((0, 0), (1, 0), (0, 0), (0, 0)))
    hi = jnp.arange(1, S + 1)
    diffs = []
    for g, win in enumerate(POOL_WINDOWS):
        lo = jnp.maximum(hi - win, 0)
        count = (hi - lo).astype(jnp.float32)[None, :, None]
        cg = csum[:, :, g]
        mean = (cg[:, hi] - cg[:, lo]) / count
        diffs.append(mean - ug[:, :, g])
    d = jnp.stack(diffs, axis=2).astype(u.dtype)
    y = jnp.einsum('bsgc,gcd->bsgd', d, pool_w)
    return y.reshape(B, S, POOL_WIDTH) * pool_scale


def stick_breaking_attention(q, k, v):
    B, S, H, Dh = q.shape
    scale = Dh ** -0.5
    qf = jnp.transpose(q, (0, 2, 1, 3)).astype(jnp.float32) * scale
    kf = jnp.transpose(k, (0, 2, 1, 3)).astype(jnp.float32)
    vf = jnp.transpose(v, (0, 2, 1, 3)).astype(jnp.float32)
    outs = []
    for i in range(S // SB_BLOCK):
        q0, q1 = i * SB_BLOCK, (i + 1) * SB_BLOCK
        z = jnp.einsum('bhqd,bhkd->bhqk', qf[:, :, q0:q1], kf[:, :, :q1])
        causal = jnp.arange(q1)[None, :] < jnp.arange(q0, q1)[:, None]
        log_keep = jnp.where(causal, jax.nn.log_sigmoid(-z), 0.0)
        later = lax.cumsum(log_keep, axis=3, reverse=True) - log_keep
        att = jnp.where(causal, jnp.exp(jax.nn.log_sigmoid(z) + later), 0.0)
        outs.append(jnp.einsum('bhqk,bhkd->bhqd', att, vf[:, :, :q1]))
    o = jnp.concatenate(outs, axis=2)
    return jnp.transpose(o, (0, 2, 1, 3)).astype(q.dtype)


def token_shift(u, mu):
    prev = jnp.pad(u, ((0, 0), (1, 0), (0, 0)))[:, :-1]
    return u + (prev - u) * mu


def rwkv7_mixer(u, w0, w2, a0, a2, g2, k_k, k_a, r_k, ln_g, ln_b):
    B, S, _ = u.shape
    W = RWKV_WIDTH
    r, k, v, dw, da, dg = jnp.split(
        u, [W, 2 * W, 3 * W, 3 * W + RWKV_DECAY_RANK, 3 * W + RWKV_DECAY_RANK + RWKV_A_RANK], axis=-1)
    w = -jax.nn.softplus(-(w0 + jnp.tanh(dw) @ w2)) - 0.5
    decay = jnp.exp(-jnp.exp(w.astype(jnp.float32)))
    a = jax.nn.sigmoid(a0 + da @ a2)
    g = jax.nn.sigmoid(dg) @ g2

    def heads(t):
        return t.reshape(B, S, RWKV_HEADS, RWKV_HEAD_DIM).astype(jnp.float32)

    kk = heads(k * k_k)
    kk = kk / jnp.maximum(jnp.sqrt(jnp.sum(kk * kk, axis=-1, keepdims=True)), 1e-12)
    k = k * (1 + (a - 1) * k_a)
    r_h, k_h, v_h, a_h, w_h = heads(r), heads(k), heads(v), heads(a), heads(decay)

    def tm(t):
        return jnp.transpose(t, (1, 0, 2, 3))

    def step(state, inp):
        r_t, w_t, k_t, v_t, kk_t, a_t = inp
        sa = jnp.einsum('bhij,bhj->bhi', state, -kk_t)
        state = (state * w_t[:, :, None, :]
                 + sa[..., :, None] * (kk_t * a_t)[..., None, :]
                 + v_t[..., :, None] * k_t[..., None, :])
        return state, jnp.einsum('bhij,bhj->bhi', state, r_t)

    s0 = jnp.zeros((B, RWKV_HEADS, RWKV_HEAD_DIM, RWKV_HEAD_DIM), jnp.float32)
    _, y = lax.scan(step, s0, (tm(r_h), tm(w_h), tm(k_h), tm(v_h), tm(kk), tm(a_h)))
    y = jnp.transpose(y, (1, 0, 2, 3))
    mu = jnp.mean(y, axis=-1, keepdims=True)
    var = jnp.mean(jnp.square(y - mu), axis=-1, keepdims=True)
    y = ((y - mu) * lax.rsqrt(var + RWKV_GN_EPS)).reshape(B, S, W) * ln_g + ln_b
    bonus = jnp.sum(r_h * k_h * r_k, axis=-1, keepdims=True) * v_h
    y = y + bonus.reshape(B, S, W)
    return (y * g).astype(u.dtype)


def causal_depthwise_conv(u, w, b):
    S = u.shape[1]
    up = jnp.pad(u, ((0, 0), (CONV_WIDTH - 1, 0), (0, 0)))
    out = up[:, 0:S] * w[0]
    for j in range(1, CONV_WIDTH):
        out = out + up[:, j:j + S] * w[j]
    return out + b


def setup_inputs(seed: int = 0) -> dict:
    key = jax.random.key(seed)
    ks = jax.random.split(key, 28)
    f32 = jnp.float32
    L = DEPTH

    def nrm(k, shape, scale):
        return jax.random.normal(k, shape, f32) * scale

    return {
        "x": nrm(ks[0], (BATCH, SEQ, D_MODEL), 1.0),
        "w_in": nrm(ks[1], (L, D_MODEL, N_IN), D_MODEL ** -0.5),
        "b_gate": nrm(ks[2], (L, N_BRANCH * D_MODEL), 0.02),
        "pool_w": nrm(ks[3], (L, POOL_GROUPS, POOL_GROUP_DIM, POOL_GROUP_DIM), POOL_GROUP_DIM ** -0.5),
        "pool_scale": 1.0 + nrm(ks[4], (L, POOL_WIDTH), 0.1),
        "rwkv_mu": jax.random.uniform(ks[5], (L, RWKV_SHIFT_WIDTH), f32),
        "rwkv_w0": jax.random.uniform(ks[6], (L, RWKV_WIDTH), f32, -6.0, 1.0),
        "rwkv_w2": nrm(ks[7], (L, RWKV_DECAY_RANK, RWKV_WIDTH), 0.1),
        "rwkv_a0": nrm(ks[8], (L, RWKV_WIDTH), 0.5),
        "rwkv_a2": nrm(ks[9], (L, RWKV_A_RANK, RWKV_WIDTH), 0.1),
        "rwkv_g2": nrm(ks[10], (L, RWKV_GATE_RANK, RWKV_WIDTH), RWKV_GATE_RANK ** -0.5),
        "rwkv_k_k": 0.85 + nrm(ks[11], (L, RWKV_WIDTH), 0.05),
        "rwkv_k_a": 1.0 + nrm(ks[12], (L, RWKV_WIDTH), 0.05),
        "rwkv_r_k": nrm(ks[13], (L, RWKV_HEADS, RWKV_HEAD_DIM), 0.1),
        "rwkv_ln_g": 1.0 + nrm(ks[14], (L, RWKV_WIDTH), 0.05),
        "rwkv_ln_b": nrm(ks[15], (L, RWKV_WIDTH), 0.02),
        "w_branch_pool": nrm(ks[16], (L, POOL_WIDTH, D_MODEL), POOL_WIDTH ** -0.5 * DEEPNORM_BETA),
        "w_branch_attn": nrm(ks[17], (L, SB_WIDTH, D_MODEL), SB_WIDTH ** -0.5 * DEEPNORM_BETA),
        "w_branch_rwkv": nrm(ks[18], (L, RWKV_WIDTH, D_MODEL), RWKV_WIDTH ** -0.5 * DEEPNORM_BETA),
        "w_out": nrm(ks[19], (L, D_MODEL, D_MODEL), D_MODEL ** -0.5 * DEEPNORM_BETA),
        "ln1_g": 1.0 + nrm(ks[20], (L, D_MODEL), 0.05),
        "ln1_b": nrm(ks[21], (L, D_MODEL), 0.02),
        "w_up": nrm(ks[22], (L, D_MODEL, 2 * D_FF), D_MODEL ** -0.5),
        "ffn_conv_w": nrm(ks[23], (L, CONV_WIDTH, D_FF), CONV_WIDTH ** -0.5),
        "ffn_conv_b": nrm(ks[24], (L, D_FF), 0.02),
        "w_down": nrm(ks[25], (L, D_FF, D_MODEL), D_FF ** -0.5 * DEEPNORM_BETA),
        "ln2_g": 1.0 + nrm(ks[26], (L, D_MODEL), 0.05),
        "ln2_b": nrm(ks[27], (L, D_MODEL), 0.02),
    }


def reference(x, w_in, b_gate, pool_w, pool_scale, rwkv_mu, rwkv_w0, rwkv_w2, rwkv_a0,
              rwkv_a2, rwkv_g2, rwkv_k_k, rwkv_k_a, rwkv_r_k, rwkv_ln_g, rwkv_ln_b,
              w_branch_pool, w_branch_attn, w_branch_rwkv, w_out, ln1_g, ln1_b,
              w_up, ffn_conv_w, ffn_conv_b, w_down, ln2_g, ln2_b):
    B, S, _ = x.shape
    for l in range(DEPTH):
        proj = x @ w_in[l]
        u_pool, q, k, v, u_rwkv, gate_logits = jnp.split(
            proj, [SPLIT_POOL, SPLIT_Q, SPLIT_K, SPLIT_V, SPLIT_RWKV], axis=-1)
        y_pool = pool_mixer(u_pool, pool_w[l], pool_scale[l])
        y_attn = stick_breaking_attention(
            q.reshape(B, S, SB_HEADS, SB_HEAD_DIM),
            k.reshape(B, S, SB_HEADS, SB_HEAD_DIM),
            v.reshape(B, S, SB_HEADS, SB_HEAD_DIM)).reshape(B, S, SB_WIDTH)
        y_rwkv = rwkv7_mixer(token_shift(u_rwkv, rwkv_mu[l]), rwkv_w0[l], rwkv_w2[l],
                             rwkv_a0[l], rwkv_a2[l], rwkv_g2[l], rwkv_k_k[l], rwkv_k_a[l],
                             rwkv_r_k[l], rwkv_ln_g[l], rwkv_ln_b[l])
        gates = jax.nn.sigmoid(gate_logits + b_gate[l]).reshape(B, S, N_BRANCH, D_MODEL)
        merged = (gates[:, :, 0] * (y_pool @ w_branch_pool[l])
                  + gates[:, :, 1] * (y_attn @ w_branch_attn[l])
                  + gates[:, :, 2] * (y_rwkv @ w_branch_rwkv[l]))
        x = layer_norm(DEEPNORM_ALPHA * x + merged @ w_out[l], ln1_g[l], ln1_b[l])
        act_in, lin = jnp.split(x @ w_up[l], [D_FF], axis=-1)
        act_in = causal_depthwise_conv(act_in, ffn_conv_w[l], ffn_conv_b[l])
        ffn = (jax.nn.gelu(act_in, approximate=False) * lin) @ w_down[l]
        x = layer_norm(DEEPNORM_ALPHA * x + ffn, ln2_g[l], ln2_b[l])
    return x
```

```python
import contextlib
import numpy as np
import ml_dtypes
import concourse.bass as bass
import concourse.mybir as mybir
from concourse.bass_utils import run_bass_kernel_spmd

F32 = mybir.dt.float32
BF16 = mybir.dt.bfloat16
AF = mybir.ActivationFunctionType
ALU = mybir.AluOpType
NPBF = ml_dtypes.bfloat16

D = 4096
B = 2
S = 4096
T_ALL = B * S
DEPTH = 2
NCORE = 8
D_FF = 11008
FF_SH = D_FF // NCORE
ALPHA = (2 * DEPTH) ** 0.25
LN_EPS = 1e-5
SPLIT_POOL = 1024
SPLIT_Q = 3072
SPLIT_K = 5120
SPLIT_V = 7168
RW = 1024
SPLIT_RWKV = SPLIT_V + 3360
GN_EPS = 64e-5


class Buf:
    __slots__ = ("w", "r", "prev")

    def __init__(self):
        self.w = {}
        self.r = {}
        self.prev = {}


def _merge(dst, src):
    for k, v in src.items():
        if dst.get(k, 0) < v:
            dst[k] = v


class Prog:
    ENG = ("tensor", "vector", "scalar", "gpsimd", "sync")

    def __init__(self, nc):
        self.nc = nc
        self.q = {e: [] for e in self.ENG}
        self.cnt = {}
        self.seen = {e: {} for e in self.ENG}
        self.marks = []

    def _deps(self, eng, reads, writes, add):
        need = {}
        for b in reads:
            _merge(need, b.w)
        for b in writes:
            if not (add and not b.r):
                nprev = {}
                _merge(nprev, b.r)
                _merge(nprev, b.w)
                b.prev = nprev
                b.w = {}
                b.r = {}
            _merge(need, b.prev)
        waits = []
        seen = self.seen[eng]
        for k, v in need.items():
            if k == "tensor" and eng == "tensor":
                continue
            if seen.get(k, 0) >= v:
                continue
            seen[k] = v
            waits.append((k, v))
        return waits

    def _post(self, key, n, reads, writes):
        for b in reads:
            if b.r.get(key, 0) < n:
                b.r[key] = n
        for b in writes:
            if b.w.get(key, 0) < n:
                b.w[key] = n

    def op(self, eng, fn, reads=(), writes=(), add=False):
        waits = self._deps(eng, reads, writes, add)
        n = self.cnt.get(eng, 0) + 1
        self.cnt[eng] = n
        self.q[eng].append((waits, fn, eng, 1))
        self._post(eng, n, reads, writes)

    def dma(self, eng, fn, reads=(), writes=(), sem="d0", add=True):
        waits = self._deps(eng, reads, writes, add)
        n = self.cnt.get(sem, 0) + 16
        self.cnt[sem] = n
        self.q[eng].append((waits, fn, sem, 16))
        self._post(sem, n, reads, writes)

    def mark(self, label):
        self.marks.append((label, self.cnt.get("tensor", 0)))

    def barrier(self):
        for e in self.ENG:
            waits = []
            for k, v in self.cnt.items():
                if k == "tensor" and e == "tensor":
                    continue
                if self.seen[e].get(k, 0) < v:
                    self.seen[e][k] = v
                    waits.append((k, v))
            if waits:
                self.q[e].append((waits, None, None, 0))

    def build(self):
        nc = self.nc
        keys = set(self.cnt.keys())
        with contextlib.ExitStack() as es:
            sems = {k: es.enter_context(nc.semaphore("s_" + k)) for k in sorted(keys)}
            block = es.enter_context(nc.Block())

            def mk(e):
                def body(eng):
                    for waits, fn, key, inc in self.q[e]:
                        for k, v in waits:
                            eng.wait_ge(sems[k], v)
                        if fn is not None:
                            fn(eng).then_inc(sems[key], inc)
                    for k, v in self.cnt.items():
                        if self.seen[e].get(k, 0) < v and e in ("sync", "gpsimd"):
                            eng.wait_ge(sems[k], v)
                return body

            block.tensor(mk("tensor"))
            block.vector(mk("vector"))
            block.scalar(mk("scalar"))
            block.gpsimd(mk("gpsimd"))
            block.sync(mk("sync"))


class Ctx:
    def __init__(self):
        self.nc = bass.Bass("TRN2", target_bir_lowering=False)
        self.P = Prog(self.nc)
        self.es = contextlib.ExitStack()
        self.n = 0
        self.bufs = {}
        self.stack = []

    def dram_in(self, name, shape, dt):
        return self.nc.dram_tensor(name, list(shape), dt, kind="ExternalInput").ap()

    def dram_out(self, name, shape, dt):
        return self.nc.dram_tensor(name, list(shape), dt, kind="ExternalOutput").ap()

    def dram(self, name, shape, dt):
        return self.nc.dram_tensor(name, list(shape), dt).ap()

    def push(self):
        self.stack.append(contextlib.ExitStack())

    def pop(self):
        self.P.barrier()
        self.stack.pop().close()

    def sb(self, shape, dt, name=None):
        self.n += 1
        es = self.stack[-1] if self.stack else self.es
        t = es.enter_context(self.nc.sbuf_tensor(f"{name or 'sb'}_{self.n}", list(shape), dt))
        return t

    def ps(self, name=None, shape=(128, 512), dt=F32):
        self.n += 1
        es = self.stack[-1] if self.stack else self.es
        return es.enter_context(self.nc.psum_tensor(f"{name or 'ps'}_{self.n}", list(shape), dt))

    def finish(self):
        self.P.build()
        self.es.close()
        return self.nc


def col_chunks(n, step=128):
    return [(c, min(step, n - c)) for c in range(0, n, step)]


def linear(cx, name, w_dram, K, x, T, chunks, epi, TT=512, CG=1024, wbufs=1, ksplits=None, psn=2,
           pre_group=None, pss=None):
    P = cx.P
    KC = K // 128
    assert K % 128 == 0
    groups = []
    cur = []
    curw = 0
    for ch in chunks:
        if cur and (curw + ch[1] > CG):
            groups.append(cur)
            cur, curw = [], 0
        cur.append(ch)
        curw += ch[1]
    if cur:
        groups.append(cur)
    gw = max(sum(n for _, n in g) for g in groups)
    wts = [(cx.sb([128, KC, gw], BF16, f"{name}_w{i}"), Buf()) for i in range(wbufs)]
    if ksplits is None:
        ksplits = [(0, KC)]
    if pss is None:
        pss = [(cx.ps(f"{name}_ps{i}"), Buf()) for i in range(psn * len(ksplits))]
    psn = len(pss)
    ntt = T // TT
    if x[0] == "dram":
        xts = [(cx.sb([128, KC, TT], BF16, f"{name}_x{i}"), Buf()) for i in range(2)]
        xdr = x[1]
        xdb = x[2]

    KD = 8

    def load_w(gi):
        g = groups[gi]
        wt, wb = wts[gi % wbufs]
        c0 = g[0][0]
        cw = sum(n for _, n in g)
        for k0 in range(0, KC, KD):
            k1 = min(KC, k0 + KD)
            src = w_dram[k0 * 128:k1 * 128, c0:c0 + cw].rearrange("(kc p) c -> p kc c", p=128)
            dst = wt[:, k0:k1, 0:cw]
            P.dma("gpsimd", (lambda e, d=dst, s=src: e.dma_start(out=d, in_=s)),
                  writes=[wb], sem=f"{name}_w{gi % wbufs}")

    def load_x(ti, slot):
        xt, xb = xts[slot]
        t0 = ti * TT
        for k0 in range(0, KC, KD):
            k1 = min(KC, k0 + KD)
            src = xdr[k0 * 128:k1 * 128, t0:t0 + TT].rearrange("(kc p) t -> p kc t", p=128)
            dst = xt[:, k0:k1, :]
            P.dma("sync", (lambda e, d=dst, s=src: e.dma_start(out=d, in_=s)),
                  reads=[xdb], writes=[xb], sem=f"{name}_x{slot}")

    step = 0
    total = len(groups) * ntt
    load_w(0)
    if x[0] == "dram":
        load_x(0, 0)
    pi = 0
    for gi, g in enumerate(groups):
        wt, wb = wts[gi % wbufs]
        gc0 = g[0][0]
        if pre_group is not None:
            pre_group(gi, g)
        for ti in range(ntt):
            t0 = ti * TT
            if x[0] == "dram":
                xt, xb = xts[step % 2]
                nxt = step + 1
                if nxt < total:
                    load_x(nxt % ntt, nxt % 2)
            else:
                xt_full, xb = x[1], x[2]
            if ti == 0 and wbufs > 1 and gi + 1 < len(groups):
                load_w(gi + 1)
            for (c0, n) in g:
                j0 = c0 - gc0
                psl, pbl = [], []
                for (ka, kb) in ksplits:
                    ps, pb = pss[pi % psn]
                    pi += 1
                    psl.append(ps)
                    pbl.append(pb)
                    for kc in range(ka, kb):
                        if x[0] == "dram":
                            rhs = xt[:, kc, :]
                        else:
                            rhs = xt_full[:, kc, t0:t0 + TT]
                        lhsT = wt[:, kc, j0:j0 + n]
                        P.op("tensor",
                             (lambda e, o=ps[0:n, 0:TT], l=lhsT, r=rhs, st=(kc == ka), sp=(kc == kb - 1):
                              e.matmul(o, l, r, start=st, stop=sp)),
                             reads=[wb, xb], writes=[pb], add=(kc > ka))
                if len(ksplits) == 1:
                    epi(c0, n, t0, TT, psl[0], pbl[0])
                else:
                    epi(c0, n, t0, TT, psl, pbl)
            step += 1
        if wbufs == 1 and gi + 1 < len(groups):
            load_w(gi + 1)


class LNState:
    pass


def ln_setup(cx, name, TT, ones=None, banks=None):
    st = LNState()
    if ones is None:
        st.ones = cx.sb([128, 128], F32, f"{name}_ones")
        st.ones_b = Buf()
        cx.P.op("vector", lambda e: e.memset(st.ones[:], 1.0), writes=[st.ones_b])
    else:
        st.ones, st.ones_b = ones
    if banks is None:
        st.ps_sum = cx.ps(f"{name}_pssum")
        st.ps_sq = cx.ps(f"{name}_pssq")
        st.ps_sum_b = Buf()
        st.ps_sq_b = Buf()
    else:
        (st.ps_sum, st.ps_sum_b), (st.ps_sq, st.ps_sq_b) = banks
    st.mean = cx.sb([128, TT], F32, f"{name}_mean")
    st.rstd = cx.sb([128, TT], F32, f"{name}_rstd")
    st.tmp = cx.sb([128, TT], F32, f"{name}_tmp")
    st.mean_b, st.rstd_b, st.tmp_b = Buf(), Buf(), Buf()
    return st


def ln_accum(cx, st, rt, rt_b, sq, sq_b, first, last, TT):
    P = cx.P
    P.op("scalar", lambda e: e.activation(out=sq[:, 0:TT], in_=rt[:, 0:TT], func=AF.Square),
         reads=[rt_b], writes=[sq_b])
    P.op("tensor", lambda e: e.matmul(st.ps_sum[:, 0:TT], st.ones[:], rt[:, 0:TT], start=first, stop=last),
         reads=[rt_b, st.ones_b], writes=[st.ps_sum_b], add=not first)
    P.op("tensor", lambda e: e.matmul(st.ps_sq[:, 0:TT], st.ones[:], sq[:, 0:TT], start=first, stop=last),
         reads=[sq_b, st.ones_b], writes=[st.ps_sq_b], add=not first)


def ln_finalize(cx, st, nfeat, TT, eps):
    P = cx.P
    inv = 1.0 / nfeat
    P.op("scalar", lambda e: e.activation(out=st.mean[:, 0:TT], in_=st.ps_sum[:, 0:TT], func=AF.Copy, scale=inv),
         reads=[st.ps_sum_b], writes=[st.mean_b])
    P.op("scalar", lambda e: e.activation(out=st.rstd[:, 0:TT], in_=st.ps_sq[:, 0:TT], func=AF.Copy, scale=inv),
         reads=[st.ps_sq_b], writes=[st.rstd_b])
    P.op("vector", lambda e: e.tensor_tensor(out=st.tmp[:, 0:TT], in0=st.mean[:, 0:TT], in1=st.mean[:, 0:TT], op=ALU.mult),
         reads=[st.mean_b], writes=[st.tmp_b])
    P.op("vector", lambda e: e.tensor_tensor(out=st.rstd[:, 0:TT], in0=st.rstd[:, 0:TT], in1=st.tmp[:, 0:TT], op=ALU.subtract),
         reads=[st.rstd_b, st.tmp_b], writes=[st.rstd_b])
    P.op("vector", lambda e: e.tensor_scalar(out=st.rstd[:, 0:TT], in0=st.rstd[:, 0:TT], scalar1=float(eps), scalar2=None,
                                             op0=ALU.add),
         reads=[st.rstd_b], writes=[st.rstd_b])
    P.op("scalar", lambda e: e.activation(out=st.rstd[:, 0:TT], in_=st.rstd[:, 0:TT], func=AF.Sqrt),
         reads=[st.rstd_b], writes=[st.rstd_b])
    P.op("vector", lambda e: e.reciprocal(out=st.rstd[:, 0:TT], in_=st.rstd[:, 0:TT]),
         reads=[st.rstd_b], writes=[st.rstd_b])


def ln_apply(cx, st, name, r_dram, r_db, nfc, t0g, TT, gb, gb_b, out32, out16, o32_b, o16_b, nslot=2):
    P = cx.P
    rts = [(cx.sb([128, TT], F32, f"{name}_r{i}"), Buf()) for i in range(nslot)]
    ys = [(cx.sb([128, TT], F32, f"{name}_y{i}"), Buf()) for i in range(nslot)]
    y16 = [(cx.sb([128, TT], BF16, f"{name}_yb{i}"), Buf()) for i in range(nslot)]
    for fc in range(nfc):
        rt, rb = rts[fc % nslot]
        y, yb = ys[fc % nslot]
        yh, yhb = y16[fc % nslot]
        rows = slice(fc * 128, (fc + 1) * 128)
        P.dma("sync", (lambda e, d=rt[:, :], s=r_dram[rows, t0g:t0g + TT]: e.dma_start(out=d, in_=s)),
              reads=[r_db], writes=[rb], sem=f"{name}_ld{fc % nslot}")
        P.op("vector", (lambda e, y=y, rt=rt: e.tensor_tensor(out=y[:, :], in0=rt[:, :], in1=st.mean[:, 0:TT], op=ALU.subtract)),
             reads=[rb, st.mean_b], writes=[yb])
        P.op("vector", (lambda e, y=y: e.tensor_tensor(out=y[:, :], in0=y[:, :], in1=st.rstd[:, 0:TT], op=ALU.mult)),
             reads=[yb, st.rstd_b], writes=[yb])
        P.op("vector", (lambda e, y=y, fc=fc: e.tensor_scalar(out=y[:, :], in0=y[:, :], scalar1=gb[:, fc, 0:1], scalar2=gb[:, fc, 1:2],
                                                             op0=ALU.mult, op1=ALU.add)),
             reads=[yb, gb_b], writes=[yb])
        if out32 is not None:
            P.dma("gpsimd", (lambda e, s=y[:, :], d=out32[rows, t0g:t0g + TT]: e.dma_start(out=d, in_=s)),
                  reads=[yb], writes=[o32_b], sem=f"{name}_st{fc % nslot}")
        if out16 is not None:
            P.op("scalar", (lambda e, yh=yh, y=y: e.activation(out=yh[:, :], in_=y[:, :], func=AF.Copy)),
                 reads=[yb], writes=[yhb])
            P.dma("gpsimd", (lambda e, s=yh[:, :], d=out16[rows, t0g:t0g + TT]: e.dma_start(out=d, in_=s)),
                  reads=[yhb], writes=[o16_b], sem=f"{name}_sh{fc % nslot}")


def residual_ln_epilogue(cx, name, sts, xres_dram, r_dram, r_db, toff, TT, nfc, xres_b=None):
    P = cx.P
    xr = [(cx.sb([128, TT], F32, f"{name}_xr{i}"), Buf()) for i in range(2)]
    rts = [(cx.sb([128, TT], F32, f"{name}_rt{i}"), Buf()) for i in range(2)]
    sqs = [(cx.sb([128, TT], F32, f"{name}_sq{i}"), Buf()) for i in range(2)]
    cnt = [0]

    def epi(c0, n, t0, tt, ps, pb):
        i = cnt[0]
        cnt[0] += 1
        fc = c0 // 128
        st = sts[t0 // TT]
        tg = toff + t0
        xt, xb = xr[i % 2]
        rt, rb = rts[i % 2]
        sq, sb_ = sqs[i % 2]
        rows = slice(c0, c0 + 128)
        P.dma("sync", (lambda e, d=xt[:, :], s=xres_dram[rows, tg:tg + TT]: e.dma_start(out=d, in_=s)),
              reads=([xres_b] if xres_b is not None else []), writes=[xb], sem=f"{name}_xr{i % 2}")
        P.op("vector", (lambda e, rt=rt, xt=xt, ps=ps: e.scalar_tensor_tensor(out=rt[:, :], in0=xt[:, :], scalar=float(ALPHA), in1=ps[:, 0:TT],
                                                                              op0=ALU.mult, op1=ALU.add)),
             reads=[xb, pb], writes=[rb])
        ln_accum(cx, st, rt, rb, sq, sb_, first=(fc == 0), last=(fc == nfc - 1), TT=TT)
        P.dma("gpsimd", (lambda e, s=rt[:, :], d=r_dram[rows, tg:tg + TT]: e.dma_start(out=d, in_=s)),
              reads=[rb], writes=[r_db], sem=f"{name}_rst{i % 2}")
    return epi


def ff_pairs(nsh):
    pairs = []
    off = 0
    for (c, n) in col_chunks(nsh):
        pairs.append((off, off + n, n, c))
        off += 2 * n
    return pairs


def build_L4(T, K=D, nsh=FF_SH, S_=S):
    cx = Ctx()
    xT = cx.dram_in("xT", [K, T], BF16)
    w = cx.dram_in("w", [K, 2 * nsh], F32)
    cw = cx.dram_in("cw", [nsh, 4], F32)
    gT = cx.dram_out("gT", [nsh, T], BF16)
    emit_L4(cx, xT, Buf(), w, cw, gT, Buf(), T, K, nsh, S_)
    return cx.finish()


def emit_L4(cx, xT, x_b, w, cw, gT, g_b, T, K=D, nsh=FF_SH, S_=S):
    P = cx.P
    P.mark("L4")
    cx.push()
    TT = 512
    pairs = ff_pairs(nsh)
    npair = len(pairs)
    cwt = cx.sb([128, npair, 4], F32, "cwt")
    cw_b = Buf()
    for i, (_, _, n, r0) in enumerate(pairs):
        P.dma("sync", (lambda e, d=cwt[0:n, i, :], s=cw[r0:r0 + n, :]: e.dma_start(out=d, in_=s)), writes=[cw_b], sem="cw")
    H = cx.sb([128, npair, 2], F32, "halo")
    H_b = [Buf() for _ in range(npair)]
    As = [(cx.sb([128, TT + 2], F32, f"A{i}"), Buf()) for i in range(2)]
    Cs = [(cx.sb([128, TT], F32, f"C{i}"), Buf()) for i in range(2)]
    GLs = [(cx.sb([128, TT], F32, f"GL{i}"), Buf()) for i in range(2)]
    Os = [(cx.sb([128, TT], BF16, f"O{i}"), Buf()) for i in range(2)]
    chunks = []
    info = {}
    for i, (ao, lo, n, r0) in enumerate(pairs):
        chunks.append((ao, n))
        chunks.append((lo, n))
        info[ao] = (i, True, r0)
        info[lo] = (i, False, r0)
    cnt = [0]
    cur = {}

    def epi(c0, n, t0, tt, ps, pb):
        i, is_act, r0 = info[c0]
        if is_act:
            k = cnt[0]
            cnt[0] += 1
            A, Ab = As[k % 2]
            C, Cb = Cs[k % 2]
            GL, GLb = GLs[k % 2]
            cur[i] = (GL, GLb)
            P.op("scalar", (lambda e: e.activation(out=A[0:n, 2:2 + TT], in_=ps[0:n, 0:TT], func=AF.Copy)),
                 reads=[pb], writes=[Ab])
            if t0 % S_ == 0:
                P.op("gpsimd", (lambda e: e.memset(A[0:n, 0:2], 0.0)), writes=[Ab], add=True)
            else:
                P.op("gpsimd", (lambda e: e.tensor_copy(out=A[0:n, 0:2], in_=H[0:n, i, :])), reads=[H_b[i]], writes=[Ab], add=True)
            P.op("gpsimd", (lambda e: e.tensor_copy(out=H[0:n, i, :], in_=A[0:n, TT:TT + 2])), reads=[Ab], writes=[H_b[i]])
            P.op("vector", (lambda e: e.tensor_scalar(out=C[0:n, :], in0=A[0:n, 2:TT + 2], scalar1=cwt[0:n, i, 2:3], scalar2=cwt[0:n, i, 3:4],
                                                      op0=ALU.mult, op1=ALU.add)), reads=[Ab, cw_b], writes=[Cb])
            P.op("vector", (lambda e: e.scalar_tensor_tensor(out=C[0:n, :], in0=A[0:n, 1:TT + 1], scalar=cwt[0:n, i, 1:2], in1=C[0:n, :],
                                                             op0=ALU.mult, op1=ALU.add)), reads=[Ab, Cb, cw_b], writes=[Cb])
            P.op("vector", (lambda e: e.scalar_tensor_tensor(out=C[0:n, :], in0=A[0:n, 0:TT], scalar=cwt[0:n, i, 0:1], in1=C[0:n, :],
                                                             op0=ALU.mult, op1=ALU.add)), reads=[Ab, Cb, cw_b], writes=[Cb])
            P.op("scalar", (lambda e: e.activation(out=GL[0:n, :], in_=C[0:n, :], func=AF.Gelu)), reads=[Cb], writes=[GLb])
        else:
            GL, GLb = cur[i]
            k = cnt[0]
            O, Ob = Os[k % 2]
            P.op("vector", (lambda e: e.tensor_tensor(out=O[0:n, :], in0=GL[0:n, :], in1=ps[0:n, 0:TT], op=ALU.mult)),
                 reads=[GLb, pb], writes=[Ob])
            P.dma("gpsimd", (lambda e: e.dma_start(out=gT[r0:r0 + n, t0:t0 + TT], in_=O[0:n, :])), reads=[Ob], writes=[g_b], sem=f"gst{k % 2}")

    linear(cx, "up", w, K, ("dram", xT, x_b), T, chunks, epi, TT=TT, CG=768, wbufs=2)
    cx.pop()


def build_L5(Tl, K=D_FF, Dm=D):
    cx = Ctx()
    gT = cx.dram_in("gT", [K, Tl], BF16)
    w = cx.dram_in("w", [K, Dm], F32)
    xres = cx.dram_in("xres", [Dm, Tl], F32)
    gbd = cx.dram_in("gb", [Dm, 2], F32)
    o32 = cx.dram_out("o32", [Dm, Tl], F32)
    o16 = cx.dram_out("o16", [Dm, Tl], BF16)
    rsc = cx.dram("rsc", [Dm, Tl], F32)
    emit_L5(cx, gT, Buf(), w, xres, Buf(), gbd, o32, Buf(), o16, Buf(), rsc, Buf(), Tl, K, Dm)
    return cx.finish()


def emit_L5(cx, gT, gT_b, w, xres, xres_b, gbd, o32, o32_b, o16, o16_b, rsc, r_db, Tl, K=D_FF, Dm=D):
    P = cx.P
    P.mark("L5")
    cx.push()
    TT = 512
    KC = K // 128
    nfc = Dm // 128
    gb = cx.sb([128, nfc, 2], F32, "gbt")
    gb_b = Buf()
    P.dma("sync", lambda e: e.dma_start(out=gb[:, :, :], in_=gbd.rearrange("(c p) t -> p c t", p=128)), writes=[gb_b], sem="gb")
    st = ln_setup(cx, "ln", TT)
    X = cx.sb([128, KC, TT], BF16, "X")
    X_b = Buf()
    chunks = col_chunks(Dm)
    for h in range(Tl // TT):
        t0g = h * TT
        for k0 in range(0, KC, 8):
            k1 = min(KC, k0 + 8)
            P.dma("sync", (lambda e, d=X[:, k0:k1, :], s=gT[k0 * 128:k1 * 128, t0g:t0g + TT].rearrange("(kc p) t -> p kc t", p=128):
                           e.dma_start(out=d, in_=s)), reads=[gT_b], writes=[X_b], sem="xin")
        cx.push()
        epi = residual_ln_epilogue(cx, "e5", [st], xres, rsc, r_db, t0g, TT, nfc, xres_b=xres_b)
        linear(cx, "dn", w, K, ("sbuf", X, X_b), TT, chunks, epi, TT=TT, CG=128, wbufs=2)
        ln_finalize(cx, st, Dm, TT, LN_EPS)
        ln_apply(cx, st, "ap5", rsc, r_db, nfc, t0g, TT, gb, gb_b, o32, o16, o32_b, o16_b)
        cx.pop()
    cx.pop()


def build_L23(Tl, Dm=D, KB=(1024, 2048, 1024)):
    cx = Ctx()
    KY = sum(KB)
    yT = cx.dram_in("yT", [KY, Tl], BF16)
    gates = cx.dram_in("gates", [3 * Dm, Tl], BF16)
    wb = cx.dram_in("wb", [KY, Dm], F32)
    wo = cx.dram_in("wo", [Dm, Dm], F32)
    xres = cx.dram_in("xres", [Dm, Tl], F32)
    gbd = cx.dram_in("gb", [Dm, 2], F32)
    o32 = cx.dram_out("o32", [Dm, Tl], F32)
    o16 = cx.dram_out("o16", [Dm, Tl], BF16)
    rsc = cx.dram("rsc", [Dm, Tl], F32)
    emit_L23(cx, yT, Buf(), gates, Buf(), wb, wo, xres, Buf(), gbd, o32, Buf(), o16, Buf(), rsc, Buf(), Tl, Dm, KB)
    return cx.finish()


def emit_L23(cx, yT, yT_b, gates, gates_b, wb, wo, xres, xres_b, gbd, o32, o32_b, o16, o16_b, rsc, r_db, Tl, Dm=D, KB=(1024, 2048, 1024)):
    P = cx.P
    P.mark("L23")
    KY = sum(KB)
    cx.push()
    TT = 512
    ntt = Tl // TT
    nfc = Dm // 128
    KCY = KY // 128
    banks = [(cx.ps(f"bank{i}"), Buf()) for i in range(8)]
    gb = cx.sb([128, nfc, 2], F32, "gbt")
    gb_b = Buf()
    P.dma("sync", lambda e: e.dma_start(out=gb[:, :, :], in_=gbd.rearrange("(c p) t -> p c t", p=128)), writes=[gb_b], sem="gb")
    M = cx.sb([128, nfc, Tl], BF16, "M")
    M_b = Buf()
    ones = cx.sb([128, 128], F32, "ones")
    ones_b = Buf()
    P.op("vector", lambda e: e.memset(ones[:], 1.0), writes=[ones_b])
    sts = [ln_setup(cx, f"ln{t}", TT, ones=(ones, ones_b), banks=(banks[2 + 2 * t], banks[3 + 2 * t])) for t in range(ntt)]
    cx.push()
    Y = cx.sb([128, KCY, Tl], BF16, "Y")
    Y_b = Buf()
    for k0 in range(0, KCY, 8):
        k1 = min(KCY, k0 + 8)
        P.dma("sync", (lambda e, d=Y[:, k0:k1, :], s=yT[k0 * 128:k1 * 128, :].rearrange("(kc p) t -> p kc t", p=128):
                       e.dma_start(out=d, in_=s)), reads=[yT_b], writes=[Y_b], sem="yin")
    gts = [(cx.sb([128, 3, TT], BF16, f"gt{i}"), Buf()) for i in range(2)]
    m0s = [(cx.sb([128, TT], F32, f"m0{i}"), Buf()) for i in range(2)]
    m1s = [(cx.sb([128, TT], F32, f"m1{i}"), Buf()) for i in range(2)]
    m2s = [(cx.sb([128, TT], F32, f"m2{i}"), Buf()) for i in range(2)]
    gview = gates.rearrange("(b f) t -> f b t", b=3)
    cnt = [0]

    def epi_b(c0, n, t0, tt, psl, pbl):
        i = cnt[0]
        cnt[0] += 1
        fc = c0 // 128
        gt, gtb = gts[i % 2]
        m0, m0b = m0s[i % 2]
        m1, m1b = m1s[i % 2]
        m2, m2b = m2s[i % 2]
        P.dma("sync", (lambda e: e.dma_start(out=gt[:, :, :], in_=gview[c0:c0 + 128, :, t0:t0 + TT])), reads=[gates_b], writes=[gtb], sem=f"gt{i % 2}")
        P.op("vector", (lambda e: e.tensor_tensor(out=m0[:, :], in0=gt[:, 0, :], in1=psl[0][:, 0:TT], op=ALU.mult)),
             reads=[gtb, pbl[0]], writes=[m0b])
        P.op("vector", (lambda e: e.tensor_tensor(out=m1[:, :], in0=gt[:, 1, :], in1=psl[1][:, 0:TT], op=ALU.mult)),
             reads=[gtb, pbl[1]], writes=[m1b])
        P.op("vector", (lambda e: e.tensor_tensor(out=m2[:, :], in0=gt[:, 2, :], in1=psl[2][:, 0:TT], op=ALU.mult)),
             reads=[gtb, pbl[2]], writes=[m2b])
        P.op("gpsimd", (lambda e: e.tensor_tensor(out=m0[:, :], in0=m0[:, :], in1=m1[:, :], op=ALU.add)),
             reads=[m0b, m1b], writes=[m0b])
        P.op("gpsimd", (lambda e: e.tensor_tensor(out=M[:, fc, t0:t0 + TT], in0=m0[:, :], in1=m2[:, :], op=ALU.add)),
             reads=[m0b, m2b], writes=[M_b], add=True)

    ks = []
    a = 0
    for kb in KB:
        ks.append((a // 128, (a + kb) // 128))
        a += kb
    linear(cx, "br", wb, KY, ("sbuf", Y, Y_b), Tl, col_chunks(Dm), epi_b, TT=TT, CG=256, wbufs=2, ksplits=ks, pss=banks[0:6])
    cx.pop()
    cx.push()
    epi = residual_ln_epilogue(cx, "eo", sts, xres, rsc, r_db, 0, TT, nfc, xres_b=xres_b)
    linear(cx, "out", wo, Dm, ("sbuf", M, M_b), Tl, col_chunks(Dm), epi, TT=TT, CG=256, wbufs=2, pss=banks[0:2])
    for t in range(ntt):
        ln_finalize(cx, sts[t], Dm, TT, LN_EPS)
    cx.pop()
    for t in range(ntt):
        cx.push()
        ln_apply(cx, sts[t], "ap23", rsc, r_db, nfc, t * TT, TT, gb, gb_b, o32, o16, o32_b, o16_b)
        cx.pop()
    cx.pop()


CST_ID, CST_TRI, CST_ONES, CST_BO = 0, 128, 256, 384
CST_AM = 512
CST_MU = CST_AM + 2048
CST_ML = CST_MU + 256
CST_MUI = CST_ML + 256
CST_I4 = CST_MUI + 256
CST_N = CST_I4 + 256
RC = 64
RW_STAGE = 4
DBG = None
DBG_ON = False


def make_consts():
    c = np.zeros((128, CST_N), np.float32)
    p = np.arange(128)[:, None]
    f = np.arange(128)[None, :]
    c[:, CST_ID:CST_ID + 128] = (p == f)
    c[:, CST_TRI:CST_TRI + 128] = (p >= f)
    c[:, CST_ONES:CST_ONES + 128] = 1.0
    c[:, CST_BO:CST_BO + 128] = (p // 64 == f // 64)
    fq = np.arange(512)[None, :]
    for j in range(4):
        c[:, CST_AM + 512 * j:CST_AM + 512 * (j + 1)] = (fq > p + 128 * j)
    s_ = np.arange(64)[:, None]
    t_ = np.arange(64)[None, :]
    for i in range(4):
        c[0:64, CST_MU + 64 * i:CST_MU + 64 * (i + 1)] = (s_ < t_)
        c[0:64, CST_ML + 64 * i:CST_ML + 64 * (i + 1)] = (s_ > t_)
        c[0:64, CST_MUI + 64 * i:CST_MUI + 64 * (i + 1)] = (s_ <= t_)
        c[0:64, CST_I4 + 64 * i:CST_I4 + 64 * (i + 1)] = (s_ == t_)
    return c


L1_LAY = [("up", 256), ("q", 256), ("k", 256), ("v", 256), ("rr", 128), ("rk", 128), ("rv", 128),
          ("dwa", 128), ("dg", 160), ("gates", 1536)]
L1_NCOL = sum(n for _, n in L1_LAY)
PP_GB = 0
PP_PSC = 12
PP_MU = 13
PP_W0 = 20
PP_A0 = 21
PP_KK = 22
PP_KA = 23
PP_RK = 24
PP_LNG = 25
PP_LNB = 26
NPP = 27


def build_L1(T, K=D, S_=S, phases=("proj", "pool", "attn", "rwkv")):
    cx = Ctx()
    P = cx.P
    xT = cx.dram_in("xT", [K, T], BF16)
    w = cx.dram_in("w", [K, L1_NCOL], F32)
    cst_d = cx.dram_in("cst", [128, CST_N], F32)
    pp_d = cx.dram_in("pp", [128, NPP], F32)
    coef_d = cx.dram_in("coef", [4, S_], F32)
    poolw_d = cx.dram_in("poolw", [256, 128], F32)
    lrw_d = cx.dram_in("lrw", [288, 128], F32)
    yT = cx.dram_out("yT", [512, T], BF16)
    gates = cx.dram_out("gates", [1536, T], BF16)
    scr = make_l1_scratch(cx, T)
    cst = cx.sb([128, CST_N], F32, "cst_sb")
    cst_b = Buf()
    P.dma("sync", lambda e: e.dma_start(out=cst[:, :], in_=cst_d[:, :]), writes=[cst_b], sem="cst")
    emit_L1(cx, cst, cst_b, xT, Buf(), w, pp_d, coef_d, poolw_d, lrw_d, yT[0:128, :], yT[128:384, :], yT[384:512, :], Buf(),
            [gates[0:512, :], gates[512:1024, :], gates[1024:1536, :]], Buf(), scr, T, K, S_, phases)
    return cx.finish()


def make_l1_scratch(cx, T, sfx=""):
    return {"upT": (cx.dram("upT" + sfx, [256, T], F32), Buf()), "qT": (cx.dram("qT" + sfx, [256, T], BF16), Buf()),
            "kT": (cx.dram("kT" + sfx, [256, T], BF16), Buf()), "vT": (cx.dram("vT" + sfx, [256, T], BF16), Buf()),
            "rwT": (cx.dram("rwT" + sfx, [384, T], F32), Buf()), "lrT": (cx.dram("lrT" + sfx, [288, T], F32), Buf())}


def emit_L1(cx, cst, cst_b, xT, x_b, w, pp_d, coef_d, poolw_d, lrw_d, yp, ya, yr, y_b, gate_aps, g_b, scr, T, K, S_,
            phases=("proj", "pool", "attn", "rwkv")):
    P = cx.P
    NB = T // S_
    upT, up_b = scr["upT"]
    qT, q_b = scr["qT"]
    kT, k_b = scr["kT"]
    vT, v_b = scr["vT"]
    rwT, rw_b = scr["rwT"]
    lrT, lr_b = scr["lrT"]
    TT = 512
    cx.push()
    pp = cx.sb([128, NPP], F32, "pp_sb")
    pp_b = Buf()
    P.dma("sync", lambda e: e.dma_start(out=pp[:, :], in_=pp_d[:, :]), writes=[pp_b], sem="cst")
    P.mark("L1.proj")
    cx.push()
    dst = {"up": (upT, 0, up_b, F32), "q": (qT, 0, q_b, BF16), "k": (kT, 0, k_b, BF16), "v": (vT, 0, v_b, BF16),
           "rr": (rwT, 0, rw_b, F32), "rk": (rwT, 128, rw_b, F32), "rv": (rwT, 256, rw_b, F32),
           "dwa": (lrT, 0, lr_b, F32), "dg": (lrT, 128, lr_b, F32), "gates": (None, 0, g_b, BF16)}
    chunks = []
    cinfo = {}
    off = 0
    for nm, wd in L1_LAY:
        for (c, n) in col_chunks(wd):
            chunks.append((off + c, n))
            cinfo[off + c] = (nm, c)
        off += wd
    st32 = [(cx.sb([128, TT], F32, f"st32_{i}"), Buf()) for i in range(2)]
    st16 = [(cx.sb([128, TT], BF16, f"st16_{i}"), Buf()) for i in range(2)]
    cnt = [0]

    def epi(c0, n, t0, tt, ps, pb):
        nm, c = cinfo[c0]
        d, r0, db, dt = dst[nm]
        i = cnt[0]
        cnt[0] += 1
        stg, sb_ = (st32 if dt == F32 else st16)[i % 2]
        if nm == "gates":
            gi = c // 128
            P.op("scalar", (lambda e: e.activation(out=stg[0:n, :], in_=ps[0:n, 0:TT], func=AF.Sigmoid, bias=pp[0:n, PP_GB + gi:PP_GB + gi + 1])),
                 reads=[pb, pp_b], writes=[sb_])
        elif nm == "q":
            P.op("scalar", (lambda e: e.activation(out=stg[0:n, :], in_=ps[0:n, 0:TT], func=AF.Copy, scale=float(128 ** -0.5))), reads=[pb], writes=[sb_])
        elif i % 2 == 0:
            P.op("scalar", (lambda e: e.activation(out=stg[0:n, :], in_=ps[0:n, 0:TT], func=AF.Copy)), reads=[pb], writes=[sb_])
        else:
            P.op("vector", (lambda e: e.tensor_copy(out=stg[0:n, :], in_=ps[0:n, 0:TT])), reads=[pb], writes=[sb_])
        if nm == "gates":
            dd_ = gate_aps[c // 512][(c % 512):(c % 512) + n, t0:t0 + TT]
        else:
            dd_ = d[r0 + c:r0 + c + n, t0:t0 + TT]
        P.dma("gpsimd", (lambda e: e.dma_start(out=dd_, in_=stg[0:n, :])), reads=[sb_], writes=[db], sem=("pst32_" if dt == F32 else "pst16_") + str(i % 2))

    if "proj" in phases:
        linear(cx, "pj", w, K, ("dram", xT, x_b), T, chunks, epi, TT=TT, CG=896, wbufs=2)
    cx.pop()

    P.mark("L1.pool")
    cx.push()
    pw = cx.sb([128, 2, 128], BF16, "pw")
    pw_b = Buf()
    P.dma("gpsimd", lambda e: e.dma_start(out=pw[:, :, :], in_=poolw_d.rearrange("(c p) n -> p c n", p=128)), writes=[pw_b], sem="pw")
    HL = 16
    Us = [(cx.sb([128, HL + TT], F32, f"pU{i}"), Buf()) for i in range(2)]
    s2 = (cx.sb([128, HL + TT], F32, "ps2"), Buf())
    s4 = (cx.sb([128, HL + TT], F32, "ps4"), Buf())
    s8 = (cx.sb([128, HL + TT], F32, "ps8"), Buf())
    s16 = (cx.sb([128, HL + TT], F32, "ps16"), Buf())
    dd = (cx.sb([128, TT], F32, "pdd"), Buf())
    tm = (cx.sb([128, TT], F32, "ptm"), Buf())
    cf = [(cx.sb([128, 4, TT], F32, f"pcf{i}"), Buf()) for i in range(2)]
    Dt = [(cx.sb([128, 2, TT], BF16, f"pD{i}"), Buf()) for i in range(2)]
    pps = [(cx.ps(f"pool_ps{i}"), Buf()) for i in range(2)]
    po = [(cx.sb([128, TT], BF16, f"po{i}"), Buf()) for i in range(2)]
    W_ = HL + TT
    for ti in range(T // TT if "pool" in phases else 0):
        t0 = ti * TT
        pos0 = t0 % S_
        cft, cfb = cf[ti % 2]
        P.dma("sync", (lambda e, cft=cft, pos0=pos0: e.dma_start(out=cft[:, :, :], in_=coef_d[:, pos0:pos0 + TT].partition_broadcast(128))),
              writes=[cfb], sem=f"pcf{ti % 2}")
        Dti, Dtb = Dt[ti % 2]
        for ic in range(2):
            U, Ub = Us[ic]
            if pos0 == 0:
                P.op("gpsimd", (lambda e, U=U: e.memset(U[:, 0:HL], 0.0)), writes=[Ub])
                P.dma("sync", (lambda e, U=U, ic=ic, t0=t0: e.dma_start(out=U[:, HL:W_], in_=upT[ic * 128:(ic + 1) * 128, t0:t0 + TT])),
                      reads=[up_b], writes=[Ub], sem=f"pU{ic}")
            else:
                P.dma("sync", (lambda e, U=U, ic=ic, t0=t0: e.dma_start(out=U[:, 0:W_], in_=upT[ic * 128:(ic + 1) * 128, t0 - HL:t0 + TT])),
                      reads=[up_b], writes=[Ub], sem=f"pU{ic}", add=False)
            prev, prevb = U, Ub
            for (sw, swb), sh in ((s2, 1), (s4, 2), (s8, 4), (s16, 8)):
                lo = 2 * sh - 1
                P.op("vector", (lambda e, sw=sw, prev=prev, lo=lo, sh=sh: e.tensor_tensor(out=sw[:, lo:W_], in0=prev[:, lo:W_], in1=prev[:, lo - sh:W_ - sh], op=ALU.add)),
                     reads=[prevb], writes=[swb])
                prev, prevb = sw, swb
            d_, db_ = dd
            t_, tb_ = tm
            P.op("vector", (lambda e, cft=cft: e.tensor_tensor(out=d_[:, :], in0=s2[0][:, HL:W_], in1=cft[:, 0, :], op=ALU.mult)), reads=[s2[1], cfb], writes=[db_])
            for wi, (sw, swb) in ((1, s4), (2, s8), (3, s16)):
                P.op("vector", (lambda e, cft=cft, sw=sw, wi=wi: e.tensor_tensor(out=t_[:, :], in0=sw[:, HL:W_], in1=cft[:, wi, :], op=ALU.mult)), reads=[swb, cfb], writes=[tb_])
                P.op("vector", (lambda e: e.tensor_tensor(out=d_[:, :], in0=d_[:, :], in1=t_[:, :], op=ALU.add)), reads=[db_, tb_], writes=[db_])
            P.op("vector", (lambda e, U=U, Dti=Dti, ic=ic: e.tensor_tensor(out=Dti[:, ic, :], in0=d_[:, :], in1=U[:, HL:W_], op=ALU.subtract)),
                 reads=[db_, Ub], writes=[Dtb], add=(ic == 1))
        ps, pb = pps[ti % 2]
        for ic in range(2):
            P.op("tensor", (lambda e, ps=ps, Dti=Dti, ic=ic: e.matmul(ps[:, 0:TT], pw[:, ic, :], Dti[:, ic, :], start=(ic == 0), stop=(ic == 1))),
                 reads=[pw_b, Dtb], writes=[pb], add=(ic == 1))
        o, ob = po[ti % 2]
        P.op("scalar", (lambda e, o=o, ps=ps: e.activation(out=o[:, :], in_=ps[:, 0:TT], func=AF.Copy, scale=pp[:, PP_PSC:PP_PSC + 1])),
             reads=[pb, pp_b], writes=[ob])
        P.dma("gpsimd", (lambda e, o=o, t0=t0: e.dma_start(out=yp[0:128, t0:t0 + TT], in_=o[:, :])), reads=[ob], writes=[y_b], sem=f"yst{ti % 2}")
    cx.pop()

    P.mark("L1.attn")
    cx.push()
    if "attn" in phases:
        emit_attention(cx, cst, cst_b, qT, kT, vT, q_b, k_b, v_b, ya, y_b, T, S_, yrow0=0)
    cx.pop()

    P.mark("L1.rwkv")
    cx.push()
    if "rwkv" in phases:
        emit_rwkv(cx, cst, cst_b, [dict(pp_d=pp_d, rwT=rwT, rw_b=rw_b, lrT=lrT, lr_b=lr_b, lrw_d=lrw_d, yT=yr, tok0=bb * S_) for bb in range(NB)], y_b, S_)
    cx.pop()
    cx.pop()


def emit_attention(cx, cst, cst_b, qT, kT, vT, q_b, k_b, v_b, yT, y_b, T, S_, HD=128, NH=2, yrow0=128):
    P = cx.P
    NB = T // S_
    QW = 512
    nqc = S_ // QW
    nkb = S_ // 128
    scale = HD ** -0.5
    kinv = float(HD ** 0.5)
    Qs = (cx.sb([128, S_], BF16, "aQ"), Buf())
    Ks = (cx.sb([128, S_], BF16, "aK"), Buf())
    Vt = (cx.sb([128, S_], BF16, "aVt"), Buf())
    Vk = (cx.sb([128, nkb, 128], BF16, "aVk"), Buf())
    idb = cx.sb([128, 128], BF16, "aidb")
    idb_b = Buf()
    P.op("vector", lambda e: e.tensor_copy(out=idb[:, :], in_=cst[:, CST_ID:CST_ID + 128]), reads=[cst_b], writes=[idb_b])
    ntri = cx.sb([128, 128], BF16, "antri")
    nones = cx.sb([128, 128], BF16, "anones")
    nt_b = Buf()
    P.op("vector", lambda e: e.tensor_scalar(out=ntri[:, :], in0=cst[:, CST_TRI:CST_TRI + 128], scalar1=-1.0, scalar2=None, op0=ALU.mult),
         reads=[cst_b], writes=[nt_b])
    P.op("vector", lambda e: e.memset(nones[:, :], -1.0), writes=[nt_b], add=True)
    amb = cx.sb([128, 4, 512], BF16, "amb")
    amb_b = Buf()
    P.op("vector", lambda e: e.tensor_copy(out=amb[:, :, :], in_=cst[:, CST_AM:CST_AM + 2048].rearrange("p (j f) -> p j f", j=4)),
         reads=[cst_b], writes=[amb_b])
    zps = [(cx.ps(f"a_z{i}"), Buf()) for i in range(2)]
    cps = [(cx.ps(f"a_c{i}"), Buf()) for i in range(2)]
    ops_ = [(cx.ps(f"a_o{i}"), Buf()) for i in range(2)]
    vps = (cx.ps("a_v"), Buf())
    NS = 3
    Es = [(cx.sb([128, QW], F32, f"aE{i}"), Buf()) for i in range(2)]
    SPs = [(cx.sb([128, QW], F32, f"aSP{i}"), Buf()) for i in range(2)]
    SPH = [(cx.sb([128, QW], BF16, f"aSPH{i}"), Buf()) for i in range(NS)]
    SPL = [(cx.sb([128, QW], BF16, f"aSPL{i}"), Buf()) for i in range(NS)]
    CSH = [(cx.sb([128, QW], BF16, f"aCSH{i}"), Buf()) for i in range(NS)]
    CSL = [(cx.sb([128, QW], BF16, f"aCSL{i}"), Buf()) for i in range(NS)]
    ATs = [(cx.sb([128, QW], BF16, f"aAT{i}"), Buf()) for i in range(NS)]
    CS = (cx.sb([128, QW], F32, "aCS"), Buf())
    OUs = [(cx.sb([128, QW], BF16, f"aOU{i}"), Buf()) for i in range(2)]
    sw = [0]
    for h in range(NH):
        for b in range(NB):
            tb = b * S_
            rows = slice(h * HD, (h + 1) * HD)
            P.dma("sync", (lambda e, rows=rows, tb=tb: e.dma_start(out=Qs[0][:, :], in_=qT[rows, tb:tb + S_])), reads=[q_b], writes=[Qs[1]], sem="aq", add=False)
            P.dma("sync", (lambda e, rows=rows, tb=tb: e.dma_start(out=Ks[0][:, :], in_=kT[rows, tb:tb + S_])), reads=[k_b], writes=[Ks[1]], sem="ak", add=False)
            P.dma("sync", (lambda e, rows=rows, tb=tb: e.dma_start(out=Vt[0][:, :], in_=vT[rows, tb:tb + S_])), reads=[v_b], writes=[Vt[1]], sem="av", add=False)
            for g0 in range(0, nkb, 4):
                g1 = min(nkb, g0 + 4)
                for kb in range(g0, g1):
                    P.op("tensor", (lambda e, kb=kb, g0=g0: e.matmul(vps[0][:, (kb - g0) * 128:(kb - g0 + 1) * 128], Vt[0][:, kb * 128:(kb + 1) * 128], idb[:, :], start=True, stop=True)),
                         reads=[Vt[1], idb_b], writes=[vps[1]], add=(kb > g0))
                P.op("scalar", (lambda e, g0=g0, g1=g1: e.activation(out=Vk[0][:, g0:g1, :], in_=vps[0][:, 0:(g1 - g0) * 128].rearrange("p (g d) -> p g d", d=128), func=AF.Copy)),
                     reads=[vps[1]], writes=[Vk[1]], add=(g0 > 0))
            tiles = []
            for qc in range(nqc):
                kbs = list(range(4 * qc + 3, -1, -1))
                for idx, kb in enumerate(kbs):
                    tiles.append(dict(qc=qc, kb=kb, j=kb - 4 * qc, first=(idx == 0), last=(idx == len(kbs) - 1)))
            nt = len(tiles)
            r0 = yrow0 + h * HD

            def stage_a(i):
                t = tiles[i]
                q0 = t["qc"] * QW
                kb, j = t["kb"], t["j"]
                z, zb = zps[i % 2]
                E, Eb = Es[i % 2]
                SP, SPb = SPs[i % 2]
                sph, sphb = SPH[i % NS]
                spl, splb = SPL[i % NS]
                P.op("tensor", (lambda e: e.matmul(z[:, 0:QW], Ks[0][:, kb * 128:(kb + 1) * 128], Qs[0][:, q0:q0 + QW], start=True, stop=True)),
                     reads=[Ks[1], Qs[1]], writes=[zb])
                P.op("scalar", (lambda e: e.activation(out=E[:, :], in_=z[:, 0:QW], func=AF.Exp)), reads=[zb], writes=[Eb])
                P.op("scalar", (lambda e: e.activation(out=SP[:, :], in_=E[:, :], func=AF.Ln, bias=1.0)), reads=[Eb], writes=[SPb])
                if j >= 0:
                    P.op("gpsimd", (lambda e: e.tensor_tensor(out=SP[:, :], in0=SP[:, :], in1=cst[:, CST_AM + 512 * j:CST_AM + 512 * (j + 1)], op=ALU.mult)),
                         reads=[SPb, cst_b], writes=[SPb])
                P.op("vector", (lambda e: e.tensor_copy(out=sph[:, :], in_=SP[:, :])), reads=[SPb], writes=[sphb])
                P.op("vector", (lambda e: e.tensor_tensor(out=spl[:, :], in0=SP[:, :], in1=sph[:, :], op=ALU.subtract)), reads=[SPb, sphb], writes=[splb])
                if not t["last"]:
                    if t["first"]:
                        P.op("vector", (lambda e: e.tensor_copy(out=CS[0][:, :], in_=SP[:, :])), reads=[SPb], writes=[CS[1]])
                    else:
                        P.op("vector", (lambda e: e.tensor_tensor(out=CS[0][:, :], in0=CS[0][:, :], in1=SP[:, :], op=ALU.add)), reads=[SPb, CS[1]], writes=[CS[1]])
                    csh, cshb = CSH[(i + 1) % NS]
                    csl, cslb = CSL[(i + 1) % NS]
                    P.op("vector", (lambda e: e.tensor_copy(out=csh[:, :], in_=CS[0][:, :])), reads=[CS[1]], writes=[cshb])
                    P.op("vector", (lambda e: e.tensor_tensor(out=csl[:, :], in0=CS[0][:, :], in1=csh[:, :], op=ALU.subtract)), reads=[CS[1], cshb], writes=[cslb])

            def stage_b(i):
                t = tiles[i]
                q0 = t["qc"] * QW
                kb, j = t["kb"], t["j"]
                c, cb = cps[i % 2]
                sph, sphb = SPH[i % NS]
                spl, splb = SPL[i % NS]
                AT, ATb = ATs[i % NS]
                first = t["first"]
                P.op("tensor", (lambda e: e.matmul(c[:, 0:QW], Ks[0][:, kb * 128:(kb + 1) * 128], Qs[0][:, q0:q0 + QW], start=True, stop=False)),
                     reads=[Ks[1], Qs[1]], writes=[cb])
                P.op("tensor", (lambda e: e.matmul(c[:, 0:QW], ntri[:, :], sph[:, :], start=False, stop=False)), reads=[nt_b, sphb], writes=[cb], add=True)
                P.op("tensor", (lambda e: e.matmul(c[:, 0:QW], ntri[:, :], spl[:, :], start=False, stop=first)), reads=[nt_b, splb], writes=[cb], add=True)
                if not first:
                    csh, cshb = CSH[i % NS]
                    csl, cslb = CSL[i % NS]
                    P.op("tensor", (lambda e: e.matmul(c[:, 0:QW], nones[:, :], csh[:, :], start=False, stop=False)), reads=[nt_b, cshb], writes=[cb], add=True)
                    P.op("tensor", (lambda e: e.matmul(c[:, 0:QW], nones[:, :], csl[:, :], start=False, stop=True)), reads=[nt_b, cslb], writes=[cb], add=True)
                P.op("scalar", (lambda e: e.activation(out=AT[:, :], in_=c[:, 0:QW], func=AF.Exp)), reads=[cb], writes=[ATb])
                if j >= 0:
                    P.op("gpsimd", (lambda e: e.tensor_tensor(out=AT[:, :], in0=AT[:, :], in1=amb[:, j, :], op=ALU.mult)), reads=[ATb, amb_b], writes=[ATb])

            def stage_c(i):
                t = tiles[i]
                q0 = t["qc"] * QW
                kb = t["kb"]
                AT, ATb = ATs[i % NS]
                if t["first"]:
                    sw[0] += 1
                ou_ps, ou_pb = ops_[sw[0] % 2]
                first, last = t["first"], t["last"]
                P.op("tensor", (lambda e: e.matmul(ou_ps[:, 0:QW], Vk[0][:, kb, :], AT[:, :], start=first, stop=last)),
                     reads=[Vk[1], ATb], writes=[ou_pb], add=(not first))
                if last:
                    OU, OUb = OUs[sw[0] % 2]
                    P.op("vector", (lambda e: e.tensor_copy(out=OU[:, :], in_=ou_ps[:, 0:QW])), reads=[ou_pb], writes=[OUb])
                    P.dma("gpsimd", (lambda e, tg=tb + q0, r0=r0: e.dma_start(out=yT[r0:r0 + HD, tg:tg + QW], in_=OU[:, :])), reads=[OUb], writes=[y_b], sem=f"yst2_{sw[0] % 2}")

            for step in range(nt + 2):
                if step < nt:
                    stage_a(step)
                if 0 <= step - 1 < nt:
                    stage_b(step - 1)
                if 0 <= step - 2 < nt:
                    stage_c(step - 2)


def emit_rwkv(cx, cst, cst_b, srcs, y_b, S_):
    P = cx.P
    NB = len(srcs)
    NI = 2 * NB
    W4 = 64 * NI
    TT = 512
    NCH = TT // RC
    NEG_E = -float(np.exp(-0.5))

    def mk(shape, dt, name):
        return (cx.sb(shape, dt, "rw_" + name), Buf())

    def V(eng, fn, reads, writes, add=False):
        P.op(eng, fn, reads=reads, writes=writes, add=add)

    banks = [(cx.ps(f"rw_bank{i}"), Buf()) for i in range(8)]
    bN, bNT, bR, bA, bB, bT, bS, bY = banks
    ident = cst[:, CST_ID:CST_ID + 128]
    BO = cst[:, CST_BO:CST_BO + 128]
    MU = cst[0:64, CST_MU:CST_MU + W4]
    ML = cst[0:64, CST_ML:CST_ML + W4]
    MUI = cst[0:64, CST_MUI:CST_MUI + W4]
    I4 = cst[0:64, CST_I4:CST_I4 + W4]

    LWs, LG0s, LG1s, omkas, pps, pp2s = [], [], [], [], [], []
    for b, sc in enumerate(srcs):
        LW = mk([128, 128], BF16, f"LW{b}")
        LG0 = mk([128, 128], BF16, f"LG0{b}")
        LG1 = mk([32, 128], BF16, f"LG1{b}")
        lrw_d = sc["lrw_d"]
        P.dma("gpsimd", lambda e, LW=LW, lrw_d=lrw_d: e.dma_start(out=LW[0][:, :], in_=lrw_d[0:128, :]), writes=[LW[1]], sem="rw_lw")
        P.dma("gpsimd", lambda e, LG0=LG0, lrw_d=lrw_d: e.dma_start(out=LG0[0][:, :], in_=lrw_d[128:256, :]), writes=[LG0[1]], sem="rw_lw")
        P.dma("gpsimd", lambda e, LG1=LG1, lrw_d=lrw_d: e.dma_start(out=LG1[0][:, :], in_=lrw_d[256:288, :]), writes=[LG1[1]], sem="rw_lw")
        ppx = mk([128, NPP], F32, f"ppx{b}")
        pp2 = mk([64, NPP], F32, f"pp2{b}")
        pp_dram = sc["pp_d"]
        P.dma("sync", lambda e, ppx=ppx, pp_dram=pp_dram: e.dma_start(out=ppx[0][:, :], in_=pp_dram[:, :]), writes=[ppx[1]], sem="rw_pp2")
        P.dma("sync", lambda e, pp2=pp2, pp_dram=pp_dram: e.dma_start(out=pp2[0][:, :], in_=pp_dram[64:128, :]), writes=[pp2[1]], sem="rw_pp2")
        omka = mk([128, 1], F32, f"omka{b}")
        V("vector", lambda e, omka=omka, ppx=ppx: e.tensor_scalar(out=omka[0][:, :], in0=ppx[0][:, PP_KA:PP_KA + 1], scalar1=-1.0, scalar2=1.0, op0=ALU.mult, op1=ALU.add),
          [ppx[1]], [omka[1]])
        LWs.append(LW); LG0s.append(LG0); LG1s.append(LG1); omkas.append(omka); pps.append(ppx); pp2s.append(pp2)
    RM = mk([128, TT], F32, "RM")
    V("vector", lambda e: e.memset(RM[0][:, :], 1.0), [], [RM[1]])
    V("vector", lambda e: e.memset(RM[0][:, :].rearrange("p (c j) -> p c j", j=RC)[:, :, 0:1], 0.0), [], [RM[1]], add=True)
    ST = mk([128, 128], F32, "ST")
    V("vector", lambda e: e.memset(ST[0][:, :], 0.0), [], [ST[1]])

    names = ["RS", "VS", "K2", "G"]
    names16 = ["RT", "AT", "BT", "KT", "BP", "KP", "VSb"]
    pb_ = [dict([(n, mk([128, TT], F32, f"{n}{b}")) for n in names] + [(n, mk([128, TT], BF16, f"{n}{b}")) for n in names16]) for b in range(NB)]
    identb = mk([128, 128], BF16, "identb")
    V("vector", lambda e: e.tensor_copy(out=identb[0][:, :], in_=cst[:, CST_ID:CST_ID + 128]), [cst_b], [identb[1]])
    GC = [mk([128, NCH], F32, f"GC{b}") for b in range(NB)]
    raw = {n: mk([128, TT + 1], F32, "raw_" + n) for n in ["r", "k", "v", "dwa", "dg0", "dg1"]}
    tmp = {n: mk([128, TT], F32, "t_" + n) for n in ["d", "ks", "dwas", "dg0s", "dg1s", "SW", "LD", "A", "KK", "BBt", "C", "CEX", "D2",
                                                      "Ein", "Eex", "Eneg", "E2", "x1", "x2", "x3"]}
    TA = mk([128, TT], BF16, "TA")
    SG0 = mk([128, TT], BF16, "SG0")
    SG1 = mk([32, TT], BF16, "SG1")
    OUT = [mk([128, TT], BF16, f"OUT{i}") for i in range(2)]
    ct = {n: mk([64, W4], (F32 if n == "XV" else BF16), "c_" + n) for n in ["N0", "NT0", "N1", "NT1", "R0", "R1", "AkT", "ArbT", "ArkT", "Vtok", "BPtok", "KPtok",
                                                                              "XV", "Xs", "Us"]}

    def shift(src, mucol, rows, dst, pos0, pp, pp_b):
        X, Xb = src
        d, db = tmp["d"]
        o, ob = dst
        V("vector", lambda e: e.tensor_tensor(out=d[0:rows, :], in0=X[0:rows, 0:TT], in1=X[0:rows, 1:TT + 1], op=ALU.subtract), [Xb], [db])
        V("vector", lambda e: e.scalar_tensor_tensor(out=o[0:rows, :], in0=d[0:rows, :], scalar=pp[0:rows, mucol:mucol + 1], in1=X[0:rows, 1:TT + 1],
                                                     op0=ALU.mult, op1=ALU.add), [db, Xb, pp_b], [ob])

    def load_raw(key, dram, db, r0, rows, tg, pos0):
        X, Xb = raw[key]
        if pos0 == 0:
            V("gpsimd", lambda e: e.memset(X[0:rows, 0:1], 0.0), [], [Xb])
            P.dma("sync", lambda e: e.dma_start(out=X[0:rows, 1:TT + 1], in_=dram[r0:r0 + rows, tg:tg + TT]), reads=[db], writes=[Xb], sem="rw_" + key)
        else:
            P.dma("sync", lambda e: e.dma_start(out=X[0:rows, 0:TT + 1], in_=dram[r0:r0 + rows, tg - 1:tg + TT]), reads=[db], writes=[Xb], sem="rw_" + key, add=False)

    def prep(b, t0):
        sc = srcs[b]
        tg = sc["tok0"] + t0
        pb = pb_[b]
        rwT, rw_b, lrT, lr_b = sc["rwT"], sc["rw_b"], sc["lrT"], sc["lr_b"]
        pp, pp_b = pps[b]
        LW, LG0, LG1, omka = LWs[b], LG0s[b], LG1s[b], omkas[b]
        load_raw("r", rwT, rw_b, 0, 128, tg, t0)
        load_raw("k", rwT, rw_b, 128, 128, tg, t0)
        load_raw("v", rwT, rw_b, 256, 128, tg, t0)
        load_raw("dwa", lrT, lr_b, 0, 128, tg, t0)
        load_raw("dg0", lrT, lr_b, 128, 128, tg, t0)
        load_raw("dg1", lrT, lr_b, 256, 32, tg, t0)
        shift(raw["r"], PP_MU + 0, 128, pb["RS"], t0, pp, pp_b)
        shift(raw["k"], PP_MU + 1, 128, tmp["ks"], t0, pp, pp_b)
        shift(raw["v"], PP_MU + 2, 128, pb["VS"], t0, pp, pp_b)
        V("scalar", lambda e: e.activation(out=pb["VSb"][0][:, :], in_=pb["VS"][0][:, :], func=AF.Copy), [pb["VS"][1]], [pb["VSb"][1]])
        shift(raw["dwa"], PP_MU + 3, 128, tmp["dwas"], t0, pp, pp_b)
        shift(raw["dg0"], PP_MU + 4, 128, tmp["dg0s"], t0, pp, pp_b)
        shift(raw["dg1"], PP_MU + 5, 32, tmp["dg1s"], t0, pp, pp_b)
        ks, ksb = tmp["ks"]
        dwas, dwasb = tmp["dwas"]
        V("scalar", lambda e: e.activation(out=TA[0][0:64, :], in_=dwas[0:64, :], func=AF.Tanh), [dwasb], [TA[1]])
        V("scalar", lambda e: e.activation(out=TA[0][64:128, :], in_=dwas[64:128, :], func=AF.Copy), [dwasb], [TA[1]], add=True)
        V("scalar", lambda e: e.activation(out=SG0[0][:, :], in_=tmp["dg0s"][0][:, :], func=AF.Sigmoid), [tmp["dg0s"][1]], [SG0[1]])
        V("scalar", lambda e: e.activation(out=SG1[0][:, :], in_=tmp["dg1s"][0][0:32, :], func=AF.Sigmoid), [tmp["dg1s"][1]], [SG1[1]])
        V("tensor", lambda e: e.matmul(bA[0][:, 0:TT], LW[0][0:64, :], TA[0][0:64, :], start=True, stop=True), [LW[1], TA[1]], [bA[1]])
        SW, SWb = tmp["SW"]
        V("scalar", lambda e: e.activation(out=SW[:, :], in_=bA[0][:, 0:TT], func=AF.Sigmoid, bias=pp[:, PP_W0:PP_W0 + 1]), [bA[1], pp_b], [SWb])
        LD, LDb = tmp["LD"]
        V("vector", lambda e: e.tensor_scalar(out=LD[:, :], in0=SW[:, :], scalar1=NEG_E, scalar2=None, op0=ALU.mult), [SWb], [LDb])
        V("tensor", lambda e: e.matmul(bB[0][:, 0:TT], LW[0][64:128, :], TA[0][64:128, :], start=True, stop=True), [LW[1], TA[1]], [bB[1]])
        A, Ab = tmp["A"]
        V("scalar", lambda e: e.activation(out=A[:, :], in_=bB[0][:, 0:TT], func=AF.Sigmoid, bias=pp[:, PP_A0:PP_A0 + 1]), [bB[1], pp_b], [Ab])
        V("tensor", lambda e: e.matmul(bT[0][:, 0:TT], LG0[0][:, :], SG0[0][:, :], start=True, stop=False), [LG0[1], SG0[1]], [bT[1]])
        V("tensor", lambda e: e.matmul(bT[0][:, 0:TT], LG1[0][:, :], SG1[0][:, :], start=False, stop=True), [LG1[1], SG1[1]], [bT[1]], add=True)
        G, Gb = pb["G"]
        V("scalar", lambda e: e.activation(out=G[:, :], in_=bT[0][:, 0:TT], func=AF.Copy), [bT[1]], [Gb])
        KK, KKb = tmp["KK"]
        x1, x1b = tmp["x1"]
        x2, x2b = tmp["x2"]
        V("vector", lambda e: e.tensor_scalar(out=KK[:, :], in0=ks[:, :], scalar1=pp[:, PP_KK:PP_KK + 1], scalar2=None, op0=ALU.mult), [ksb, pp_b], [KKb])
        V("scalar", lambda e: e.activation(out=x1[:, :], in_=KK[:, :], func=AF.Square), [KKb], [x1b])
        V("tensor", lambda e: e.matmul(bA[0][:, 0:TT], BO, x1[:, :], start=True, stop=True), [x1b, cst_b], [bA[1]])
        V("scalar", lambda e: e.activation(out=x2[:, :], in_=bA[0][:, 0:TT], func=AF.Sqrt), [bA[1]], [x2b])
        V("vector", lambda e: e.tensor_scalar(out=x2[:, :], in0=x2[:, :], scalar1=1e-12, scalar2=None, op0=ALU.max), [x2b], [x2b])
        V("vector", lambda e: e.reciprocal(out=x2[:, :], in_=x2[:, :]), [x2b], [x2b])
        V("vector", lambda e: e.tensor_tensor(out=KK[:, :], in0=KK[:, :], in1=x2[:, :], op=ALU.mult), [KKb, x2b], [KKb])
        K2, K2b = pb["K2"]
        V("vector", lambda e: e.tensor_scalar(out=x1[:, :], in0=A[:, :], scalar1=pp[:, PP_KA:PP_KA + 1], scalar2=omka[0][:, 0:1], op0=ALU.mult, op1=ALU.add),
          [Ab, pp_b, omka[1]], [x1b])
        V("vector", lambda e: e.tensor_tensor(out=K2[:, :], in0=ks[:, :], in1=x1[:, :], op=ALU.mult), [ksb, x1b], [K2b])
        BBt, BBb = tmp["BBt"]
        V("gpsimd", lambda e: e.tensor_tensor(out=BBt[:, :], in0=KK[:, :], in1=A[:, :], op=ALU.mult), [KKb, Ab], [BBb])
        C, Cb = tmp["C"]
        V("vector", lambda e: e.tensor_tensor_scan(out=C[:, :], data0=RM[0][:, :], data1=LD[:, :], initial=0.0, op0=ALU.mult, op1=ALU.add), [RM[1], LDb], [Cb])
        CEX, CEXb = tmp["CEX"]
        V("gpsimd", lambda e: e.tensor_tensor(out=CEX[:, :], in0=C[:, :], in1=LD[:, :], op=ALU.subtract), [Cb, LDb], [CEXb])
        D2, D2b = tmp["D2"]
        for cj in range(NCH):
            cs = slice(cj * RC, (cj + 1) * RC)
            ce = (cj + 1) * RC - 1
            V("vector", lambda e, cs=cs, ce=ce: e.tensor_scalar(out=D2[:, cs], in0=C[:, cs], scalar1=C[:, ce:ce + 1], scalar2=-1.0, op0=ALU.subtract, op1=ALU.mult),
              [Cb], [D2b], add=(cj > 0))
        Ein, Einb = tmp["Ein"]
        Eex, Eexb = tmp["Eex"]
        Eneg, Enegb = tmp["Eneg"]
        E2, E2b = tmp["E2"]
        V("scalar", lambda e: e.activation(out=Ein[:, :], in_=C[:, :], func=AF.Exp), [Cb], [Einb])
        V("scalar", lambda e: e.activation(out=Eex[:, :], in_=CEX[:, :], func=AF.Exp), [CEXb], [Eexb])
        V("scalar", lambda e: e.activation(out=Eneg[:, :], in_=C[:, :], func=AF.Exp, scale=-1.0), [Cb], [Enegb])
        V("scalar", lambda e: e.activation(out=E2[:, :], in_=D2[:, :], func=AF.Exp), [D2b], [E2b])
        RS, RSb = pb["RS"]
        V("vector", lambda e: e.tensor_tensor(out=pb["RT"][0][:, :], in0=RS[:, :], in1=Ein[:, :], op=ALU.mult), [RSb, Einb], [pb["RT"][1]])
        V("vector", lambda e: e.scalar_tensor_tensor(out=pb["AT"][0][:, :], in0=KK[:, :], scalar=-1.0, in1=Eex[:, :], op0=ALU.mult, op1=ALU.mult),
          [KKb, Eexb], [pb["AT"][1]])
        V("gpsimd", lambda e: e.tensor_tensor(out=pb["BT"][0][:, :], in0=BBt[:, :], in1=Eneg[:, :], op=ALU.mult), [BBb, Enegb], [pb["BT"][1]])
        V("gpsimd", lambda e: e.tensor_tensor(out=pb["KT"][0][:, :], in0=K2[:, :], in1=Eneg[:, :], op=ALU.mult), [K2b, Enegb], [pb["KT"][1]])
        V("vector", lambda e: e.tensor_tensor(out=pb["BP"][0][:, :], in0=BBt[:, :], in1=E2[:, :], op=ALU.mult), [BBb, E2b], [pb["BP"][1]])
        V("gpsimd", lambda e: e.tensor_tensor(out=pb["KP"][0][:, :], in0=K2[:, :], in1=E2[:, :], op=ALU.mult), [K2b, E2b], [pb["KP"][1]])
        V("vector", lambda e: e.tensor_copy(out=GC[b][0][:, :], in_=Ein[:, :].rearrange("p (c j) -> p c j", j=RC)[:, :, RC - 1]), [Einb], [GC[b][1]])

    def inst_list():
        return [(b, h, 2 * b + h) for b in range(NB) for h in range(2)]

    SEL = cst[:, CST_ID + 64:CST_ID + 128]
    SELb = identb[0][:, 64:128]
    ones64 = cst[0:64, CST_ONES:CST_ONES + 64]
    XK = ["RT", "AT", "BT", "KT", "RS", "K2", "VS", "G"]
    XK16 = ("RT", "AT", "BT", "KT")
    H1 = [{n: mk([64, TT], (BF16 if n in XK16 else F32), f"H1{n}{b}") for n in XK} for b in range(NB)]
    GC1 = [mk([64, NCH], F32, f"GC1{b}") for b in range(NB)]
    ST4 = mk([64, W4], F32, "ST4")
    ST4b = mk([64, W4], BF16, "ST4b")
    V("vector", lambda e: e.memset(ST4[0][:, :], 0.0), [], [ST4[1]])
    V("vector", lambda e: e.memset(ST4b[0][:, :], 0.0), [], [ST4b[1]])
    YA = mk([64, NI, TT], F32, "YA")

    def extract(b):
        for n_, key in enumerate(XK):
            t, tb = pb_[b][key]
            bank = bA if n_ % 2 == 0 else bB
            sel, selb = (SELb, identb[1]) if key in XK16 else (SEL, cst_b)
            V("tensor", lambda e, t=t, bank=bank, sel=sel: e.matmul(bank[0][0:64, 0:TT], sel, t[:, :], start=True, stop=True), [tb, selb], [bank[1]])
            d_ = H1[b][key]
            if n_ % 2 == 0:
                V("scalar", lambda e, d_=d_, bank=bank: e.activation(out=d_[0][:, :], in_=bank[0][0:64, 0:TT], func=AF.Copy), [bank[1]], [d_[1]])
            else:
                V("vector", lambda e, d_=d_, bank=bank: e.tensor_copy(out=d_[0][:, :], in_=bank[0][0:64, 0:TT]), [bank[1]], [d_[1]])
        V("tensor", lambda e: e.matmul(bT[0][0:64, 0:NCH], SEL, GC[b][0][:, :], start=True, stop=True), [GC[b][1], cst_b], [bT[1]])
        V("vector", lambda e: e.tensor_copy(out=GC1[b][0][:, :], in_=bT[0][0:64, 0:NCH]), [bT[1]], [GC1[b][1]])

    def fmt(b, key, h):
        if h == 0:
            t, tb = pb_[b][key]
        else:
            t, tb = H1[b][key]
        return t, tb

    def fm(b, key, h, cj):
        t, tb = fmt(b, key, h)
        return t[0:64, cj * RC:(cj + 1) * RC], tb

    def mm_inst(bank, lk, rk, cj):
        for n_, (b, h, i) in enumerate(inst_list()):
            l, lb = fm(b, lk, h, cj)
            r, rb = fm(b, rk, h, cj)
            V("tensor", lambda e, l=l, r=r, i=i: e.matmul(bank[0][0:64, 64 * i:64 * i + 64], l, r, start=True, stop=True),
              [lb, rb], [bank[1]], add=(n_ > 0))

    def mm_tok(bank, lt, rt_):
        for n_, (b, h, i) in enumerate(inst_list()):
            V("tensor", lambda e, i=i: e.matmul(bank[0][0:64, 64 * i:64 * i + 64], lt[0][:, 64 * i:64 * i + 64], rt_[0][:, 64 * i:64 * i + 64], start=True, stop=True),
              [lt[1], rt_[1]], [bank[1]], add=(n_ > 0))

    def evac_mask(bank, mask, dst, eng="vector"):
        V(eng, lambda e: e.tensor_tensor(out=dst[0][:, :], in0=bank[0][0:64, 0:W4], in1=mask, op=ALU.mult), [bank[1], cst_b], [dst[1]])

    def chunk_step(cj):
        mm_inst(bN, "BT", "AT", cj)
        mm_inst(bNT, "AT", "BT", cj)
        Pc, PTc = ct["N0"], ct["NT0"]
        evac_mask(bN, MU, Pc)
        evac_mask(bNT, ML, PTc)
        Rc = ct["R0"]
        V("vector", lambda e, Rc=Rc, Pc=Pc: e.tensor_tensor(out=Rc[0][:, :], in0=Pc[0][:, :], in1=I4, op=ALU.add), [Pc[1], cst_b], [Rc[1]])
        for k in range(5):
            Pn, PTn = (ct["N1"], ct["NT1"]) if k % 2 == 0 else (ct["N0"], ct["NT0"])
            Rn = ct["R1"] if k % 2 == 0 else ct["R0"]
            mm_tok(bN, PTc, Pc)
            mm_tok(bNT, Pc, PTc)
            V("scalar", lambda e, Pn=Pn: e.activation(out=Pn[0][:, :], in_=bN[0][0:64, 0:W4], func=AF.Copy), [bN[1]], [Pn[1]])
            V("vector", lambda e, PTn=PTn: e.tensor_copy(out=PTn[0][:, :], in_=bNT[0][0:64, 0:W4]), [bNT[1]], [PTn[1]])
            mm_tok(bR, PTn, Rc)
            V("vector", lambda e, Rn=Rn, Rc=Rc: e.tensor_tensor(out=Rn[0][:, :], in0=Rc[0][:, :], in1=bR[0][0:64, 0:W4], op=ALU.add), [Rc[1], bR[1]], [Rn[1]])
            Pc, PTc, Rc = Pn, PTn, Rn
        mm_inst(bA, "KT", "AT", cj)
        evac_mask(bA, MU, ct["AkT"])
        mm_inst(bB, "BT", "RT", cj)
        evac_mask(bB, MUI, ct["ArbT"])
        mm_inst(bA, "KT", "RT", cj)
        evac_mask(bA, MUI, ct["ArkT"])
        for key, dstk, eng in (("VSb", "Vtok", "scalar"), ("BP", "BPtok", "vector"), ("KP", "KPtok", "scalar")):
            for b in range(NB):
                t, tb = pb_[b][key]
                V("tensor", lambda e, t=t, b=b: e.matmul(bT[0][0:64, 128 * b:128 * b + 128], t[:, cj * RC:(cj + 1) * RC], identb[0][:, :], start=True, stop=True),
                  [tb, identb[1]], [bT[1]], add=(b > 0))
            d_ = ct[dstk]
            if eng == "scalar":
                V("scalar", lambda e, d_=d_: e.activation(out=d_[0][:, :], in_=bT[0][0:64, 0:W4], func=AF.Copy), [bT[1]], [d_[1]])
            else:
                V("vector", lambda e, d_=d_: e.tensor_copy(out=d_[0][:, :], in_=bT[0][0:64, 0:W4]), [bT[1]], [d_[1]])
        mm_tok(bB, ct["AkT"], ct["Vtok"])
        V("scalar", lambda e: e.activation(out=ct["XV"][0][:, :], in_=bB[0][0:64, 0:W4], func=AF.Copy), [bB[1]], [ct["XV"][1]])
        if RW_STAGE < 3:
            return
        for n_, (b, h, i) in enumerate(inst_list()):
            l, lb = fm(b, "AT", h, cj)
            V("tensor", lambda e, l=l, i=i: e.matmul(bS[0][0:64, 64 * i:64 * i + 64], l, ST4b[0][:, 64 * i:64 * i + 64], start=True, stop=True),
              [lb, ST4b[1]], [bS[1]], add=(n_ > 0))
        V("vector", lambda e: e.tensor_tensor(out=ct["Xs"][0][:, :], in0=ct["XV"][0][:, :], in1=bS[0][0:64, 0:W4], op=ALU.add), [ct["XV"][1], bS[1]], [ct["Xs"][1]])
        mm_tok(bS, Rc, ct["Xs"])
        V("vector", lambda e: e.tensor_copy(out=ct["Us"][0][:, :], in_=bS[0][0:64, 0:W4]), [bS[1]], [ct["Us"][1]])
        for n_, (b, h, i) in enumerate(inst_list()):
            r, rb = fm(b, "RT", h, cj)
            o = bY[0][0:64, 64 * i:64 * i + 64]
            V("tensor", lambda e, o=o, r=r, i=i: e.matmul(o, ST4b[0][:, 64 * i:64 * i + 64], r, start=True, stop=False),
              [ST4b[1], rb], [bY[1]], add=(n_ > 0))
            V("tensor", lambda e, o=o, i=i: e.matmul(o, ct["Us"][0][:, 64 * i:64 * i + 64], ct["ArbT"][0][:, 64 * i:64 * i + 64], start=False, stop=False),
              [ct["Us"][1], ct["ArbT"][1]], [bY[1]], add=True)
            V("tensor", lambda e, o=o, i=i: e.matmul(o, ct["Vtok"][0][:, 64 * i:64 * i + 64], ct["ArkT"][0][:, 64 * i:64 * i + 64], start=False, stop=True),
              [ct["Vtok"][1], ct["ArkT"][1]], [bY[1]], add=True)
        V("scalar", lambda e: e.activation(out=YA[0][:, :, cj * RC:(cj + 1) * RC], in_=bY[0][0:64, 0:W4].rearrange("p (i t) -> p i t", i=NI), func=AF.Copy),
          [bY[1]], [YA[1]], add=(cj > 0))
        for n_, (b, h, i) in enumerate(inst_list()):
            o = bS[0][0:64, 256 + 64 * i:256 + 64 * i + 64]
            V("tensor", lambda e, o=o, i=i: e.matmul(o, ct["BPtok"][0][:, 64 * i:64 * i + 64], ct["Us"][0][:, 64 * i:64 * i + 64], start=True, stop=False),
              [ct["BPtok"][1], ct["Us"][1]], [bS[1]], add=True)
            V("tensor", lambda e, o=o, i=i: e.matmul(o, ct["KPtok"][0][:, 64 * i:64 * i + 64], ct["Vtok"][0][:, 64 * i:64 * i + 64], start=False, stop=True),
              [ct["KPtok"][1], ct["Vtok"][1]], [bS[1]], add=True)
        for (b, h, i) in inst_list():
            gc = GC[b] if h == 0 else GC1[b]
            V("vector", lambda e, i=i, gc=gc: e.scalar_tensor_tensor(out=ST4[0][:, 64 * i:64 * i + 64], in0=ST4[0][:, 64 * i:64 * i + 64], scalar=gc[0][0:64, cj:cj + 1],
                                                                    in1=bS[0][0:64, 256 + 64 * i:256 + 64 * i + 64], op0=ALU.mult, op1=ALU.add),
              [ST4[1], gc[1], bS[1]], [ST4[1]])
        V("scalar", lambda e: e.activation(out=ST4b[0][:, :], in_=ST4[0][:, :], func=AF.Copy), [ST4[1]], [ST4b[1]])

    def post(b, h, t0, k):
        tg = srcs[b]["tok0"] + t0
        yT = srcs[b]["yT"]
        i = 2 * b + h
        ppt, ppb = pps[b] if h == 0 else pp2s[b]
        Y = YA[0][:, i, :]
        Yb = YA[1]
        RSt, RSb = fmt(b, "RS", h)
        K2t, K2b = fmt(b, "K2", h)
        VSt, VSb = fmt(b, "VS", h)
        Gt, Gb = fmt(b, "G", h)
        x1, x1b = tmp["x1"]
        x2, x2b = tmp["x2"]
        x3, x3b = tmp["x3"]
        x1, x2, x3 = x1[0:64, :], x2[0:64, :], x3[0:64, :]
        V("tensor", lambda e: e.matmul(bA[0][0:64, 0:TT], ones64, Y, start=True, stop=True), [Yb, cst_b], [bA[1]])
        V("scalar", lambda e: e.activation(out=x1, in_=Y, func=AF.Square), [Yb], [x1b])
        V("tensor", lambda e: e.matmul(bB[0][0:64, 0:TT], ones64, x1, start=True, stop=True), [x1b, cst_b], [bB[1]])
        V("scalar", lambda e: e.activation(out=x2, in_=bA[0][0:64, 0:TT], func=AF.Copy, scale=1.0 / 64), [bA[1]], [x2b])
        V("scalar", lambda e: e.activation(out=x3, in_=bB[0][0:64, 0:TT], func=AF.Copy, scale=1.0 / 64), [bB[1]], [x3b])
        V("vector", lambda e: e.tensor_tensor(out=x1, in0=x2, in1=x2, op=ALU.mult), [x2b], [x1b])
        V("vector", lambda e: e.tensor_tensor(out=x3, in0=x3, in1=x1, op=ALU.subtract), [x3b, x1b], [x3b])
        V("vector", lambda e: e.tensor_scalar(out=x3, in0=x3, scalar1=float(GN_EPS), scalar2=None, op0=ALU.add), [x3b], [x3b])
        V("scalar", lambda e: e.activation(out=x3, in_=x3, func=AF.Sqrt), [x3b], [x3b])
        V("vector", lambda e: e.reciprocal(out=x3, in_=x3), [x3b], [x3b])
        V("vector", lambda e: e.tensor_tensor(out=x2, in0=Y, in1=x2, op=ALU.subtract), [Yb, x2b], [x2b])
        V("vector", lambda e: e.tensor_tensor(out=x2, in0=x2, in1=x3, op=ALU.mult), [x2b, x3b], [x2b])
        V("vector", lambda e: e.tensor_scalar(out=x2, in0=x2, scalar1=ppt[0:64, PP_LNG:PP_LNG + 1], scalar2=ppt[0:64, PP_LNB:PP_LNB + 1], op0=ALU.mult, op1=ALU.add),
          [x2b, ppb], [x2b])
        V("vector", lambda e: e.tensor_tensor(out=x1, in0=RSt[0:64, :], in1=K2t[0:64, :], op=ALU.mult), [RSb, K2b], [x1b])
        V("vector", lambda e: e.tensor_scalar(out=x1, in0=x1, scalar1=ppt[0:64, PP_RK:PP_RK + 1], scalar2=None, op0=ALU.mult), [x1b, ppb], [x1b])
        V("tensor", lambda e: e.matmul(bA[0][0:64, 0:TT], ones64, x1, start=True, stop=True), [x1b, cst_b], [bA[1]])
        V("vector", lambda e: e.tensor_tensor(out=x3, in0=VSt[0:64, :], in1=bA[0][0:64, 0:TT], op=ALU.mult), [VSb, bA[1]], [x3b])
        V("vector", lambda e: e.tensor_tensor(out=x2, in0=x2, in1=x3, op=ALU.add), [x2b, x3b], [x2b])
        O, Ob = OUT[k % 2]
        V("vector", lambda e: e.tensor_tensor(out=O[0:64, :], in0=x2, in1=Gt[0:64, :], op=ALU.mult), [x2b, Gb], [Ob])
        r0 = 64 * h
        P.dma("gpsimd", lambda e, yT=yT, r0=r0, tg=tg: e.dma_start(out=yT[r0:r0 + 64, tg:tg + TT], in_=O[0:64, :]), reads=[Ob], writes=[y_b], sem=f"yst3_{k % 2}")

    k = 0
    for t0 in range(0, S_, TT):
        for b in range(NB):
            prep(b, t0)
            extract(b)
            if DBG is not None and b == 0 and t0 == 0:
                for j_, key in enumerate(["RS", "VS", "K2", "G", "RT", "AT", "BT", "KT", "BP", "KP"]):
                    t_, tb_ = pb_[0][key]
                    P.dma("sync", lambda e, t_=t_, j_=j_: e.dma_start(out=DBG[j_, :, :], in_=t_[:, :]), reads=[tb_], writes=[Buf()], sem="dbg")
                for j_, key in enumerate(["RT", "AT", "BT", "KT"]):
                    t_, tb_ = H1[0][key]
                    P.dma("sync", lambda e, t_=t_, j_=j_: e.dma_start(out=DBG[10 + j_, 0:64, :], in_=t_[:, :]), reads=[tb_], writes=[Buf()], sem="dbg")
        for cj in range(NCH if RW_STAGE >= 2 else 0):
            chunk_step(cj)
        if DBG is not None and t0 == 0:
            for j_, key in enumerate(["N0", "R1", "AkT", "ArbT", "ArkT", "Vtok", "BPtok", "KPtok", "XV", "Xs", "Us"]):
                t_, tb_ = ct[key]
                P.dma("sync", lambda e, t_=t_, j_=j_: e.dma_start(out=DBG[14 + j_, 0:64, 0:256], in_=t_[:, :]), reads=[tb_], writes=[Buf()], sem="dbg")
            P.dma("sync", lambda e: e.dma_start(out=DBG[25, 0:64, :], in_=YA[0][:, 0, :]), reads=[YA[1]], writes=[Buf()], sem="dbg")
            P.dma("sync", lambda e: e.dma_start(out=DBG[26, 0:64, 0:256], in_=ST4[0][:, :]), reads=[ST4[1]], writes=[Buf()], sem="dbg")
        for (b, h, i) in (inst_list() if RW_STAGE >= 4 else []):
            post(b, h, t0, k)
            k += 1


def build_L0(Tl, Dm=D):
    cx = Ctx()
    P = cx.P
    xin = cx.dram_in("xin", [Dm, Tl], F32)
    xo = cx.dram_out("xo", [Dm, Tl], BF16)
    ob = Buf()
    ts = [(cx.sb([128, Tl], BF16, f"c{i}"), Buf()) for i in range(4)]
    for fc in range(Dm // 128):
        t, tb = ts[fc % 4]
        rows = slice(fc * 128, (fc + 1) * 128)
        P.dma("gpsimd", (lambda e, t=t, rows=rows: e.dma_start(out=t[:, :], in_=xin[rows, :])), writes=[tb], sem=f"l0i{fc % 4}", add=False)
        P.dma("sync", (lambda e, t=t, rows=rows: e.dma_start(out=xo[rows, :], in_=t[:, :])), reads=[tb], writes=[ob], sem=f"l0o{fc % 4}")
    return cx.finish()


_PROGS = {}


def _prog(name, fn):
    if name not in _PROGS:
        _PROGS[name] = fn()
    return _PROGS[name]


def _run(nc, maps):
    res = run_bass_kernel_spmd(nc, maps, core_ids=list(range(NCORE)))
    return res.results


NVC = 8
TLB = 1024


def build_fused(T=S):
    cx = Ctx()
    P = cx.P
    xin = cx.dram_in("xin", [D, T], F32)
    cst_d = cx.dram_in("cst", [128, CST_N], F32)
    out = cx.dram_out("out", [D, T], F32)
    Lw = []
    for l in range(DEPTH):
        Lw.append({
            "w1": cx.dram_in(f"w1_{l}", [NVC, D, L1_NCOL], F32),
            "pp": cx.dram_in(f"pp_{l}", [NVC, 128, NPP], F32),
            "coef": cx.dram_in(f"coef_{l}", [NVC, 4, T], F32),
            "poolw": cx.dram_in(f"poolw_{l}", [NVC, 256, 128], F32),
            "lrw": cx.dram_in(f"lrw_{l}", [NVC, 288, 128], F32),
            "wb": cx.dram_in(f"wb_{l}", [D, D], F32),
            "wo": cx.dram_in(f"wo_{l}", [D, D], F32),
            "gb1": cx.dram_in(f"gb1_{l}", [D, 2], F32),
            "w4": cx.dram_in(f"w4_{l}", [NVC, D, 2 * FF_SH], F32),
            "cw": cx.dram_in(f"cw_{l}", [NVC, FF_SH, 4], F32),
            "wd": cx.dram_in(f"wd_{l}", [D_FF, D], F32),
            "gb2": cx.dram_in(f"gb2_{l}", [D, 2], F32),
        })
    X16 = (cx.dram("X16", [D, T], BF16), Buf())
    YT = (cx.dram("YT", [D, T], BF16), Buf())
    GT = (cx.dram("GT", [3 * D, T], BF16), Buf())
    X1_32 = (cx.dram("X1_32", [D, T], F32), Buf())
    X1_16 = (cx.dram("X1_16", [D, T], BF16), Buf())
    GFT = (cx.dram("GFT", [D_FF, T], BF16), Buf())
    X2_32 = (cx.dram("X2_32", [D, T], F32), Buf())
    rsc = (cx.dram("rscf", [D, TLB], F32), Buf())
    scrs = [make_l1_scratch(cx, T, sfx=f"_{i}") for i in range(2)]
    cst = cx.sb([128, CST_N], F32, "cst_sb")
    cst_b = Buf()
    P.dma("sync", lambda e: e.dma_start(out=cst[:, :], in_=cst_d[:, :]), writes=[cst_b], sem="cst")
    cx.push()
    ts = [(cx.sb([128, T], BF16, f"c{i}"), Buf()) for i in range(4)]
    for fc in range(D // 128):
        t, tb = ts[fc % 4]
        rows = slice(fc * 128, (fc + 1) * 128)
        P.dma("gpsimd", (lambda e, t=t, rows=rows: e.dma_start(out=t[:, :], in_=xin[rows, :])), writes=[tb], sem=f"l0i{fc % 4}", add=False)
        P.dma("sync", (lambda e, t=t, rows=rows: e.dma_start(out=X16[0][rows, :], in_=t[:, :])), reads=[tb], writes=[X16[1]], sem=f"l0o{fc % 4}")
    cx.pop()
    xres = (xin, Buf())
    for l in range(DEPTH):
        W = Lw[l]
        for vc in range(NVC):
            g, hf = vc // 2, vc % 2
            emit_L1(cx, cst, cst_b, X16[0], X16[1], W["w1"][vc], W["pp"][vc], W["coef"][vc], W["poolw"][vc], W["lrw"][vc],
                    YT[0][256 * g + 128 * hf:256 * g + 128 * hf + 128, :], YT[0][1024 + 256 * vc:1024 + 256 * vc + 256, :],
                    YT[0][3072 + 128 * vc:3072 + 128 * vc + 128, :], YT[1],
                    [GT[0][D * b + 512 * vc:D * b + 512 * vc + 512, :] for b in range(3)], GT[1], scrs[vc % 2], T, D, T,
                    phases=("proj", "pool", "attn"))
            if vc % 2 == 1:
                P.mark("L1.rwkv")
                cx.push()
                emit_rwkv(cx, cst, cst_b,
                          [dict(pp_d=W["pp"][v], rwT=scrs[v % 2]["rwT"][0], rw_b=scrs[v % 2]["rwT"][1], lrT=scrs[v % 2]["lrT"][0],
                                lr_b=scrs[v % 2]["lrT"][1], lrw_d=W["lrw"][v], yT=YT[0][3072 + 128 * v:3072 + 128 * v + 128, :], tok0=0)
                           for v in (vc - 1, vc)], YT[1], T)
                cx.pop()
        for tb_ in range(T // TLB):
            tk = slice(tb_ * TLB, (tb_ + 1) * TLB)
            emit_L23(cx, YT[0][:, tk], YT[1], GT[0][:, tk], GT[1], W["wb"], W["wo"], xres[0][:, tk], xres[1], W["gb1"],
                     X1_32[0][:, tk], X1_32[1], X1_16[0][:, tk], X1_16[1], rsc[0], rsc[1], TLB)
        for vc in range(NVC):
            emit_L4(cx, X1_16[0], X1_16[1], W["w4"][vc], W["cw"][vc], GFT[0][FF_SH * vc:FF_SH * (vc + 1), :], GFT[1], T, D, FF_SH, T)
        last = (l == DEPTH - 1)
        for tb_ in range(T // TLB):
            tk = slice(tb_ * TLB, (tb_ + 1) * TLB)
            if last:
                emit_L5(cx, GFT[0][:, tk], GFT[1], W["wd"], X1_32[0][:, tk], X1_32[1], W["gb2"], out[:, tk], Buf(), None, Buf(),
                        rsc[0], rsc[1], TLB)
            else:
                emit_L5(cx, GFT[0][:, tk], GFT[1], W["wd"], X1_32[0][:, tk], X1_32[1], W["gb2"], X2_32[0][:, tk], X2_32[1],
                        X16[0][:, tk], X16[1], rsc[0], rsc[1], TLB)
        xres = X2_32
    return cx.finish()


def _l1_cols(c):
    g = c // 2
    return np.concatenate([
        np.arange(256 * g, 256 * g + 256),
        SPLIT_POOL + 256 * c + np.arange(256), SPLIT_Q + 256 * c + np.arange(256), SPLIT_K + 256 * c + np.arange(256),
        SPLIT_V + 128 * c + np.arange(128), SPLIT_V + 1024 + 128 * c + np.arange(128), SPLIT_V + 2048 + 128 * c + np.arange(128),
        SPLIT_V + 3072 + np.arange(128), SPLIT_V + 3200 + np.arange(160)] +
        [SPLIT_RWKV + 4096 * b + 512 * c + np.arange(512) for b in range(3)])


def _l1_pp(c, l, b_gate, pool_scale, rwkv_mu, rwkv_w0, rwkv_a0, rwkv_k_k, rwkv_k_a, rwkv_r_k, rwkv_ln_g, rwkv_ln_b):
    A = np.asarray
    g, hf = c // 2, c % 2
    pp = np.zeros((128, NPP), np.float32)
    bg = A(b_gate[l])
    for b in range(3):
        for j in range(4):
            pp[:, PP_GB + 4 * b + j] = bg[4096 * b + 512 * c + 128 * j:4096 * b + 512 * c + 128 * j + 128]
    pp[:, PP_PSC] = A(pool_scale[l])[256 * g + 128 * hf:256 * g + 128 * hf + 128]
    mu = A(rwkv_mu[l])
    pp[:, PP_MU + 0] = mu[128 * c:128 * c + 128]
    pp[:, PP_MU + 1] = mu[1024 + 128 * c:1024 + 128 * c + 128]
    pp[:, PP_MU + 2] = mu[2048 + 128 * c:2048 + 128 * c + 128]
    pp[:, PP_MU + 3] = mu[3072:3200]
    pp[:, PP_MU + 4] = mu[3200:3328]
    pp[0:32, PP_MU + 5] = mu[3328:3360]
    sl = slice(128 * c, 128 * c + 128)
    pp[:, PP_W0] = A(rwkv_w0[l])[sl]
    pp[:, PP_A0] = A(rwkv_a0[l])[sl]
    pp[:, PP_KK] = A(rwkv_k_k[l])[sl]
    pp[:, PP_KA] = A(rwkv_k_a[l])[sl]
    pp[:, PP_RK] = A(rwkv_r_k[l]).reshape(-1)[sl]
    pp[:, PP_LNG] = A(rwkv_ln_g[l])[sl]
    pp[:, PP_LNB] = A(rwkv_ln_b[l])[sl]
    return pp


def kernel(x, w_in, b_gate, pool_w, pool_scale, rwkv_mu, rwkv_w0, rwkv_w2, rwkv_a0, rwkv_a2, rwkv_g2, rwkv_k_k, rwkv_k_a,
           rwkv_r_k, rwkv_ln_g, rwkv_ln_b, w_branch_pool, w_branch_attn, w_branch_rwkv, w_out, ln1_g, ln1_b,
           w_up, ffn_conv_w, ffn_conv_b, w_down, ln2_g, ln2_b):
    A = np.asarray
    x = A(x)
    pos = np.arange(S)
    wins = (2, 4, 8, 16)
    common = {"cst": make_consts()}
    for l in range(DEPTH):
        wi = A(w_in[l])
        common[f"w1_{l}"] = np.stack([wi[:, _l1_cols(c)] for c in range(NVC)], 0)
        common[f"pp_{l}"] = np.stack([_l1_pp(c, l, b_gate, pool_scale, rwkv_mu, rwkv_w0, rwkv_a0, rwkv_k_k, rwkv_k_a, rwkv_r_k,
                                             rwkv_ln_g, rwkv_ln_b) for c in range(NVC)], 0)
        coef = np.zeros((NVC, 4, S), np.float32)
        for c in range(NVC):
            coef[c, c // 2] = (1.0 / np.minimum(pos + 1, wins[c // 2])).astype(np.float32)
        common[f"coef_{l}"] = coef
        common[f"poolw_{l}"] = np.stack([A(pool_w[l])[c // 2][:, 128 * (c % 2):128 * (c % 2) + 128] for c in range(NVC)], 0)
        common[f"lrw_{l}"] = np.stack([np.concatenate([A(rwkv_w2[l])[:, 128 * c:128 * c + 128], A(rwkv_a2[l])[:, 128 * c:128 * c + 128],
                                                       A(rwkv_g2[l])[:, 128 * c:128 * c + 128]], 0) for c in range(NVC)], 0)
        common[f"wb_{l}"] = np.ascontiguousarray(np.concatenate([A(w_branch_pool[l]), A(w_branch_attn[l]), A(w_branch_rwkv[l])], 0))
        common[f"wo_{l}"] = A(w_out[l])
        common[f"gb1_{l}"] = np.ascontiguousarray(np.stack([A(ln1_g[l]), A(ln1_b[l])], 1))
        wu = A(w_up[l])
        cwl = A(ffn_conv_w[l])
        cbl = A(ffn_conv_b[l])
        w4 = []
        cw = []
        for c in range(NVC):
            cols = []
            for (ao, lo, n, rr) in ff_pairs(FF_SH):
                cols.append(FF_SH * c + rr + np.arange(n))
                cols.append(D_FF + FF_SH * c + rr + np.arange(n))
            w4.append(wu[:, np.concatenate(cols)])
            ch = slice(FF_SH * c, FF_SH * (c + 1))
            cw.append(np.stack([cwl[0, ch], cwl[1, ch], cwl[2, ch], cbl[ch]], 1))
        common[f"w4_{l}"] = np.stack(w4, 0)
        common[f"cw_{l}"] = np.stack(cw, 0)
        common[f"wd_{l}"] = A(w_down[l])
        common[f"gb2_{l}"] = np.ascontiguousarray(np.stack([A(ln2_g[l]), A(ln2_b[l])], 1))
    maps = []
    for b in range(B):
        m = dict(common)
        m["xin"] = np.ascontiguousarray(x[b].T)
        maps.append(m)
    nc = _prog("fused", lambda: build_fused(S))
    res = run_bass_kernel_spmd(nc, maps, core_ids=list(range(B))).results
    return np.stack([np.ascontiguousarray(res[b]["out"].T) for b in range(B)], 0).astype(np.float32)


def kernel_unfused(x, w_in, b_gate, pool_w, pool_scale, rwkv_mu, rwkv_w0, rwkv_w2, rwkv_a0, rwkv_a2, rwkv_g2, rwkv_k_k, rwkv_k_a,
           rwkv_r_k, rwkv_ln_g, rwkv_ln_b, w_branch_pool, w_branch_attn, w_branch_rwkv, w_out, ln1_g, ln1_b,
           w_up, ffn_conv_w, ffn_conv_b, w_down, ln2_g, ln2_b):
    A = lambda a: np.asarray(a)
    x = A(x)
    TL = T_ALL // NCORE
    tok = [slice(c * TL, (c + 1) * TL) for c in range(NCORE)]
    xT = np.ascontiguousarray(x.reshape(T_ALL, D).T)
    xres = [np.ascontiguousarray(xT[:, tok[c]]) for c in range(NCORE)]
    r0 = _run(_prog("L0", lambda: build_L0(TL)), [{"xin": xres[c]} for c in range(NCORE)])
    x16 = np.concatenate([r0[c]["xo"] for c in range(NCORE)], axis=1)
    cst = make_consts()
    pos = np.arange(S)
    wins = (2, 4, 8, 16)
    for l in range(DEPTH):
        wi = A(w_in[l])
        maps = []
        for c in range(NCORE):
            g, hf = c // 2, c % 2
            cols = np.concatenate([
                np.arange(256 * g, 256 * g + 256),
                SPLIT_POOL + 256 * c + np.arange(256), SPLIT_Q + 256 * c + np.arange(256), SPLIT_K + 256 * c + np.arange(256),
                SPLIT_V + 128 * c + np.arange(128), SPLIT_V + 1024 + 128 * c + np.arange(128), SPLIT_V + 2048 + 128 * c + np.arange(128),
                SPLIT_V + 3072 + np.arange(128), SPLIT_V + 3200 + np.arange(160)] +
                [SPLIT_RWKV + 4096 * b + 512 * c + np.arange(512) for b in range(3)])
            pp = np.zeros((128, NPP), np.float32)
            bg = A(b_gate[l])
            for b in range(3):
                for j in range(4):
                    pp[:, PP_GB + 4 * b + j] = bg[4096 * b + 512 * c + 128 * j:4096 * b + 512 * c + 128 * j + 128]
            pp[:, PP_PSC] = A(pool_scale[l])[256 * g + 128 * hf:256 * g + 128 * hf + 128]
            mu = A(rwkv_mu[l])
            pp[:, PP_MU + 0] = mu[128 * c:128 * c + 128]
            pp[:, PP_MU + 1] = mu[1024 + 128 * c:1024 + 128 * c + 128]
            pp[:, PP_MU + 2] = mu[2048 + 128 * c:2048 + 128 * c + 128]
            pp[:, PP_MU + 3] = mu[3072:3200]
            pp[:, PP_MU + 4] = mu[3200:3328]
            pp[0:32, PP_MU + 5] = mu[3328:3360]
            sl = slice(128 * c, 128 * c + 128)
            pp[:, PP_W0] = A(rwkv_w0[l])[sl]
            pp[:, PP_A0] = A(rwkv_a0[l])[sl]
            pp[:, PP_KK] = A(rwkv_k_k[l])[sl]
            pp[:, PP_KA] = A(rwkv_k_a[l])[sl]
            pp[:, PP_RK] = A(rwkv_r_k[l]).reshape(-1)[sl]
            pp[:, PP_LNG] = A(rwkv_ln_g[l])[sl]
            pp[:, PP_LNB] = A(rwkv_ln_b[l])[sl]
            coef = np.zeros((4, S), np.float32)
            coef[g] = (1.0 / np.minimum(pos + 1, wins[g])).astype(np.float32)
            maps.append({
                "xT": x16, "w": np.ascontiguousarray(wi[:, cols]), "cst": cst, "pp": pp, "coef": coef,
                "poolw": np.ascontiguousarray(A(pool_w[l])[g][:, 128 * hf:128 * hf + 128]),
                "lrw": np.ascontiguousarray(np.concatenate([A(rwkv_w2[l])[:, sl], A(rwkv_a2[l])[:, sl], A(rwkv_g2[l])[:, sl]], 0)),
            })
        r1 = _run(_prog("L1", lambda: build_L1(T_ALL)), maps)
        del maps
        YT = np.empty((D, T_ALL), NPBF)
        GT = np.empty((3, D, T_ALL), NPBF)
        for c in range(NCORE):
            g, hf = c // 2, c % 2
            y = r1[c]["yT"]
            YT[256 * g + 128 * hf:256 * g + 128 * hf + 128] = y[0:128]
            YT[1024 + 256 * c:1024 + 256 * c + 256] = y[128:384]
            YT[3072 + 128 * c:3072 + 128 * c + 128] = y[384:512]
            gg = r1[c]["gates"]
            for b in range(3):
                GT[b, 512 * c:512 * c + 512] = gg[512 * b:512 * b + 512]
        del r1
        GT = GT.reshape(3 * D, T_ALL)
        wb = np.ascontiguousarray(np.concatenate([A(w_branch_pool[l]), A(w_branch_attn[l]), A(w_branch_rwkv[l])], 0))
        wo = A(w_out[l])
        gb1 = np.ascontiguousarray(np.stack([A(ln1_g[l]), A(ln1_b[l])], 1))
        maps = [{"yT": np.ascontiguousarray(YT[:, tok[c]]), "gates": np.ascontiguousarray(GT[:, tok[c]]), "wb": wb, "wo": wo,
                 "xres": xres[c], "gb": gb1} for c in range(NCORE)]
        r2 = _run(_prog("L23", lambda: build_L23(TL)), maps)
        del maps, YT, GT
        x1res = [r2[c]["o32"] for c in range(NCORE)]
        x1_16 = np.concatenate([r2[c]["o16"] for c in range(NCORE)], axis=1)
        del r2
        wu = A(w_up[l])
        cwl = A(ffn_conv_w[l])
        cbl = A(ffn_conv_b[l])
        maps = []
        for c in range(NCORE):
            cols = []
            for (ao, lo, n, rr) in ff_pairs(FF_SH):
                cols.append(FF_SH * c + rr + np.arange(n))
                cols.append(D_FF + FF_SH * c + rr + np.arange(n))
            cols = np.concatenate(cols)
            ch = slice(FF_SH * c, FF_SH * (c + 1))
            cw = np.ascontiguousarray(np.stack([cwl[0, ch], cwl[1, ch], cwl[2, ch], cbl[ch]], 1))
            maps.append({"xT": x1_16, "w": np.ascontiguousarray(wu[:, cols]), "cw": cw})
        r4 = _run(_prog("L4", lambda: build_L4(T_ALL)), maps)
        del maps
        GFT = np.concatenate([r4[c]["gT"] for c in range(NCORE)], axis=0)
        del r4
        wd = A(w_down[l])
        gb2 = np.ascontiguousarray(np.stack([A(ln2_g[l]), A(ln2_b[l])], 1))
        maps = [{"gT": np.ascontiguousarray(GFT[:, tok[c]]), "w": wd, "xres": x1res[c], "gb": gb2} for c in range(NCORE)]
        r5 = _run(_prog("L5", lambda: build_L5(TL)), maps)
        del maps, GFT
        xres = [r5[c]["o32"] for c in range(NCORE)]
        x16 = np.concatenate([r5[c]["o16"] for c in range(NCORE)], axis=1)
        del r5
    outT = np.concatenate(xres, axis=1)
    return np.ascontiguousarray(outT.T).reshape(B, S, D).astype(np.float32)
```
